# Optimizing a Trainium2 kernel written in Bass

```python
import jax, jax.numpy as jnp
from jax import lax
import numpy as np

D_MODEL = 1024
BATCH = 8
SEQ = 4096
DEPTH = 4

N_MEM = 256
HEAD_DIM = 64
EPS = 1e-6
A_HEADS = 8
A_WIDTH = A_HEADS * HEAD_DIM
CHUNK = 128
POOL_WINDOWS = (2, 4, 8, 16)
B_GROUPS = len(POOL_WINDOWS)
B_GROUP_DIM = 128
B_WIDTH = B_GROUPS * B_GROUP_DIM
C_Q_HEADS = 8
C_KV_HEADS = 2
C_GROUP = C_Q_HEADS // C_KV_HEADS
C_Q_WIDTH = C_Q_HEADS * HEAD_DIM
C_KV_WIDTH = C_KV_HEADS * HEAD_DIM
WINDOW = 128
BLOCK = 128
D_WIDTH = 512
CONV_WIDTH = 31
M_HEADS = 4
M_WIDTH = M_HEADS * HEAD_DIM
OUT_WIDTH = A_WIDTH + B_WIDTH + M_WIDTH
EVEN_SIZES = (A_WIDTH, A_WIDTH, A_WIDTH, B_WIDTH, B_WIDTH, M_WIDTH, M_WIDTH)
ODD_SIZES = (C_Q_WIDTH, C_KV_WIDTH, C_KV_WIDTH, C_Q_WIDTH, D_WIDTH, D_WIDTH, D_WIDTH, M_WIDTH, M_WIDTH)
EVEN_IN = sum(EVEN_SIZES)
ODD_IN = sum(ODD_SIZES)
N_EVEN = (DEPTH + 1) // 2
N_ODD = DEPTH // 2

kernel_name = "hybrid_gmlp_pool_swa_conv_memory_trunk"


def _split(z, sizes):
    idx = list(np.cumsum(sizes)[:-1])
    return jnp.split(z, idx, axis=-1)


def rms_norm(x, g):
    xf = x.astype(jnp.float32)
    y = xf * lax.rsqrt(jnp.mean(xf * xf, axis=-1, keepdims=True) + EPS)
    return (y * g.astype(jnp.float32)).astype(x.dtype)


def layer_norm(x, g, b):
    xf = x.astype(jnp.float32)
    mu = jnp.mean(xf, axis=-1, keepdims=True)
    xc = xf - mu
    y = xc * lax.rsqrt(jnp.mean(xc * xc, axis=-1, keepdims=True) + EPS)
    return (y * g.astype(jnp.float32) + b.astype(jnp.float32)).astype(x.dtype)


def chunked_sgu(u, v, ln_g, ln_b, w_s, b_s):
    bn, s, _ = u.shape
    nc = s // CHUNK
    vn = layer_norm(v, ln_g, ln_b).reshape(bn, nc, CHUNK, A_HEADS, HEAD_DIM)
    mask = jnp.tril(jnp.ones((CHUNK, CHUNK), dtype=bool))[None]
    w = jnp.where(mask, w_s, jnp.zeros_like(w_s)).astype(vn.dtype)
    sg = jnp.einsum('hts,bnshd->bnthd', w, vn) + b_s.T.astype(vn.dtype)[None, None, :, :, None]
    return u * sg.reshape(bn, s, A_WIDTH)


def multiscale_pool(xb, w_g, scale):
    bn, s, _ = xb.shape
    xg = xb.reshape(bn, s, B_GROUPS, B_GROUP_DIM)
    xf = xg.astype(jnp.float32)
    c = jnp.cumsum(xf, axis=1)
    t = jnp.arange(s)
    outs = []
    for g, w in enumerate(POOL_WINDOWS):
        cg = c[:, :, g]
        prev = jnp.pad(cg, ((0, 0), (w, 0), (0, 0)))[:, :s]
        cnt = jnp.minimum(t + 1, w).astype(jnp.float32)[None, :, None]
        outs.append((cg - prev) / cnt)
    pooled = jnp.stack(outs, axis=2)
    diff = (pooled - xf).astype(xb.dtype)
    y = jnp.einsum('bsgc,gcd->bsgd', diff, w_g)
    return y.reshape(bn, s, B_WIDTH) * scale


def swa_sink_attention(q, k, v, qn, kn, sink):
    bn, s, _ = q.shape
    nb = s // BLOCK
    q = rms_norm(q.reshape(bn, s, C_Q_HEADS, HEAD_DIM), qn)
    k = rms_norm(k.reshape(bn, s, C_KV_HEADS, HEAD_DIM), kn)
    v = v.reshape(bn, s, C_KV_HEADS, HEAD_DIM)
    qb = q.reshape(bn, nb, BLOCK, C_KV_HEADS, C_GROUP, HEAD_DIM)

    def band(z):
        zb = z.reshape(bn, nb, BLOCK, C_KV_HEADS, HEAD_DIM)
        prev = jnp.pad(zb, ((0, 0), (1, 0), (0, 0), (0, 0), (0, 0)))[:, :nb]
        return jnp.concatenate([prev, zb], axis=2)

    kb, vb = band(k), band(v)
    sc = jnp.einsum('bnqkgd,bnskd->bnkgqs', qb, kb).astype(jnp.float32) * (HEAD_DIM ** -0.5)
    qi = jnp.arange(BLOCK)[:, None]
    kj = jnp.arange(2 * BLOCK)[None, :]
    rel = qi + BLOCK - kj
    key_abs = jnp.arange(nb)[:, None] * BLOCK + jnp.arange(2 * BLOCK)[None, :] - BLOCK
    valid = ((rel >= 0) & (rel < WINDOW))[None] & (key_abs >= 0)[:, None, :]
    sc = jnp.where(valid[None, :, None, None], sc, jnp.finfo(jnp.float32).min)
    sk = sink.astype(jnp.float32).reshape(C_KV_HEADS, C_GROUP)[None, None, :, :, None]
    m = jnp.maximum(jnp.max(sc, axis=-1), sk)
    p = jnp.exp(sc - m[..., None])
    denom = jnp.sum(p, axis=-1) + jnp.exp(sk - m)
    p = (p / denom[..., None]).astype(vb.dtype)
    o = jnp.einsum('bnkgqs,bnskd->bnqkgd', p, vb)
    return o.reshape(bn, s, C_Q_WIDTH)


def conformer_conv(a, b, dw, dw_b, ln_g, ln_b, pw):
    h = a * jax.nn.sigmoid(b)
    h = lax.conv_general_dilated(h, dw[:, None, :].astype(h.dtype), window_strides=(1,),
                                 padding=[(CONV_WIDTH - 1, 0)],
                                 dimension_numbers=('NWC', 'WIO', 'NWC'),
                                 feature_group_count=D_WIDTH) + dw_b
    h = jax.nn.silu(layer_norm(h, ln_g, ln_b))
    return h @ pw


def memory_attention(q, mem_n, w_kv, qn, kn):
    bn, s, _ = q.shape
    q = rms_norm(q.reshape(bn, s, M_HEADS, HEAD_DIM), qn)
    k, v = jnp.split(mem_n @ w_kv, 2, axis=-1)
    k = rms_norm(k.reshape(bn, N_MEM, M_HEADS, HEAD_DIM), kn)
    v = v.reshape(bn, N_MEM, M_HEADS, HEAD_DIM)
    sc = jnp.einsum('bshd,bmhd->bhsm', q, k).astype(jnp.float32) * (HEAD_DIM ** -0.5)
    p = jax.nn.softmax(sc, axis=-1).astype(v.dtype)
    o = jnp.einsum('bhsm,bmhd->bshd', p, v)
    return o.reshape(bn, s, M_WIDTH)


def setup_inputs(seed: int = 0) -> dict:
    key = jax.random.key(seed)
    ks = jax.random.split(key, 32)
    nrm = lambda k, shape, s: jax.random.normal(k, shape, jnp.float32) * s
    gain = lambda k, shape: 1.0 + 0.02 * jax.random.normal(k, shape, jnp.float32)
    return {
        "x": nrm(ks[0], (BATCH, SEQ, D_MODEL), 1.0),
        "mem": nrm(ks[1], (BATCH, N_MEM, D_MODEL), 1.0),
        "norm_g": gain(ks[2], (DEPTH, D_MODEL)),
        "mem_norm_g": gain(ks[3], (D_MODEL,)),
        "w_mem_kv": nrm(ks[4], (DEPTH, D_MODEL, 2 * M_WIDTH), D_MODEL ** -0.5),
        "m_qnorm": gain(ks[5], (DEPTH, HEAD_DIM)),
        "m_knorm": gain(ks[6], (DEPTH, HEAD_DIM)),
        "w_in_even": nrm(ks[7], (N_EVEN, D_MODEL, EVEN_IN), D_MODEL ** -0.5),
        "w_out_even": nrm(ks[8], (N_EVEN, OUT_WIDTH, D_MODEL), 0.5 * OUT_WIDTH ** -0.5),
        "a_ln_g": gain(ks[9], (N_EVEN, A_WIDTH)),
        "a_ln_b": nrm(ks[10], (N_EVEN, A_WIDTH), 0.02),
        "a_ws": nrm(ks[11], (N_EVEN, A_HEADS, CHUNK, CHUNK), CHUNK ** -0.5),
        "a_bs": gain(ks[12], (N_EVEN, A_HEADS, CHUNK)),
        "b_w": nrm(ks[13], (N_EVEN, B_GROUPS, B_GROUP_DIM, B_GROUP_DIM), B_GROUP_DIM ** -0.5),
        "b_scale": gain(ks[14], (N_EVEN, B_WIDTH)),
        "w_in_odd": nrm(ks[15], (N_ODD, D_MODEL, ODD_IN), D_MODEL ** -0.5),
        "w_out_odd": nrm(ks[16], (N_ODD, OUT_WIDTH, D_MODEL), 0.5 * OUT_WIDTH ** -0.5),
        "c_qnorm": gain(ks[17], (N_ODD, HEAD_DIM)),
        "c_knorm": gain(ks[18], (N_ODD, HEAD_DIM)),
        "c_sink": nrm(ks[19], (N_ODD, C_Q_HEADS), 1.0),
        "d_dw": nrm(ks[20], (N_ODD, CONV_WIDTH, D_WIDTH), CONV_WIDTH ** -0.5),
        "d_dw_b": nrm(ks[21], (N_ODD, D_WIDTH), 0.02),
        "d_ln_g": gain(ks[22], (N_ODD, D_WIDTH)),
        "d_ln_b": nrm(ks[23], (N_ODD, D_WIDTH), 0.02),
        "d_pw": nrm(ks[24], (N_ODD, D_WIDTH, D_WIDTH), D_WIDTH ** -0.5),
    }


def reference(x, mem, norm_g, mem_norm_g, w_mem_kv, m_qnorm, m_knorm,
              w_in_even, w_out_even, a_ln_g, a_ln_b, a_ws, a_bs, b_w, b_scale,
              w_in_odd, w_out_odd, c_qnorm, c_knorm, c_sink,
              d_dw, d_dw_b, d_ln_g, d_ln_b, d_pw):
    mem_n = rms_norm(mem, mem_norm_g)
    h = x
    for layer in range(DEPTH):
        i = layer // 2
        xn = rms_norm(h, norm_g[layer])
        if layer % 2 == 0:
            z = xn @ w_in_even[i]
            u, v, ga, xb, gb, qm, gm = _split(z, EVEN_SIZES)
            ya = chunked_sgu(u, v, a_ln_g[i], a_ln_b[i], a_ws[i], a_bs[i]) * jax.nn.silu(ga)
            yb = multiscale_pool(xb, b_w[i], b_scale[i]) * jax.nn.silu(gb)
            ym = memory_attention(qm, mem_n, w_mem_kv[layer], m_qnorm[layer], m_knorm[layer]) * jax.nn.silu(gm)
            y = jnp.concatenate([ya, yb, ym], axis=-1) @ w_out_even[i]
        else:
            z = xn @ w_in_odd[i]
            qc, kc, vc, gc, da, db, gd, qm, gm = _split(z, ODD_SIZES)
            yc = swa_sink_attention(qc, kc, vc, c_qnorm[i], c_knorm[i], c_sink[i]) * jax.nn.silu(gc)
            yd = conformer_conv(da, db, d_dw[i], d_dw_b[i], d_ln_g[i], d_ln_b[i], d_pw[i]) * jax.nn.silu(gd)
            ym = memory_attention(qm, mem_n, w_mem_kv[layer], m_qnorm[layer], m_knorm[layer]) * jax.nn.silu(gm)
            y = jnp.concatenate([yc, yd, ym], axis=-1) @ w_out_odd[i]
        h = h + y
    return h
```

```python
import numpy as np
import concourse.bass as bass
import concourse.mybir as mybir
from concourse.bass_utils import run_bass_kernel_spmd

F32 = mybir.dt.float32
BF16 = mybir.dt.bfloat16
ALU = mybir.AluOpType
AF = mybir.ActivationFunctionType

D = 1024
S = 4096
KC = 8
ST = 512
NST = S // ST
NCORES = 8
EPS = 1e-6
NMEM = 256

R_NORM, R_MEMG, R_ALNG, R_ALNB, R_BSC, R_DWB, R_DLNG, R_DLNB = 0, 4, 5, 7, 9, 11, 13, 15
R_MQ, R_MK, R_CQ, R_CK, R_SINK, R_DW = 17, 21, 25, 27, 29, 32
CF_IDENT, CF_ONES, CF_RC16, CF_IDH, CF_N = 0, 128, 256, 320, 448
CB_MCUR, CB_MPREV, CB_B64, CB_O1024, CB_O512, CB_OPAD = 0, 128, 256, 384, 512, 640
CB_IDB, CB_NCUR, CB_NPREV, CB_PCUR, CB_PPREV, CB_N = 896, 1024, 1152, 1280, 1792, 2304
EVEN_G = {"u": 0, "v": 1, "ga": 2, "xb": 3, "gb": 4, "m": 5}
ODD_G = {"q": 0, "gc": 1, "da": 2, "db": 3, "gd": 4, "m": 5, "kv": 6}
WIN_BASE = [0, 6, 13, 19]


class Buf:
    __slots__ = ("name", "w", "r")

    def __init__(self, name):
        self.name = name
        self.w = None
        self.r = []


class Sched:
    ENG = ("pe", "dve", "act", "pool", "sp")

    def __init__(self, nc):
        self.nc = nc
        self.q = {e: [] for e in self.ENG}
        self.sem = {e: nc.alloc_semaphore("c_" + e) for e in ("pe", "dve", "act", "pool")}
        self.cnt = {e: 0 for e in self.sem}
        self.known = {e: {} for e in self.ENG}
        self.dsem = {}
        self.dtot = {}
        self.final = []
        self.nins = 0

    def _deps(self, eng, R, W):
        need = {}

        def add(ev, raw):
            key, sem, val = ev
            if key == eng and eng == "pe":
                return
            if self.known[eng].get(key, 0) >= val:
                return
            if key not in need or need[key][1] < val:
                need[key] = (sem, val)

        for b in R:
            if b.w is not None:
                add(b.w, True)
        for b in W:
            if b.w is not None:
                add(b.w, False)
            for ev in b.r:
                add(ev, False)
        for key, (sem, val) in need.items():
            self.known[eng][key] = val
        return list(need.values())

    def _record(self, ev, R, W):
        for b in R:
            b.r.append(ev)
        for b in W:
            b.w = ev
            b.r = []

    def op(self, eng, fn, R=(), W=(), inc=True):
        waits = self._deps(eng, R, W)
        sem = self.sem[eng]
        if inc:
            self.cnt[eng] += 1
            val = self.cnt[eng]
        else:
            assert eng == "pe"
            val = self.cnt[eng] + 1
        self._record((eng, sem, val), R, W)
        self.q[eng].append((waits, fn, (sem, 1) if inc else None))
        self.nins += 1

    def dma(self, queue, key, fn, R=(), W=()):
        waits = self._deps(queue, R, W)
        if key not in self.dsem:
            self.dsem[key] = self.nc.alloc_semaphore("d_" + key)
            self.dtot[key] = 0
        self.dtot[key] += 16
        ev = (("d", key), self.dsem[key], self.dtot[key])
        self._record(ev, R, W)
        self.q[queue].append((waits, fn, (self.dsem[key], 16)))
        self.nins += 1
        return ev

    def emit(self, eng, e):
        for waits, fn, inc in self.q[eng]:
            for sem, val in waits:
                e.wait_ge(sem, val)
            ins = fn(e)
            if inc is not None:
                ins.then_inc(inc[0], inc[1])


def build_program(layers):
    nc = bass.Bass("TRN2", target_bir_lowering=False)

    def din(name, shape):
        return nc.dram_tensor(name, list(shape), F32, kind="ExternalInput").ap()

    x_d = din("x", (S, D))
    mem_d = din("mem", (NMEM, D))
    prow_d = din("prow", (128, D))
    cstf_d = din("cstf", (128, CF_N))
    cstb_d = din("cstb", (128, CB_N))
    winp_d = din("winp", (26, 128, KC * 512))
    woutp_d = din("woutp", (16, 128, 10 * 256))
    wkvp_d = din("wkvp", (4, 128, KC * 512))
    pwp_d = din("pwp", (2, 128, 4 * 512))
    bwp_d = din("bwp", (2, 128, 4 * 128))
    wstp_d = din("wstp", (2, 128, 8 * 128))
    absb_d = din("absb", (2, 128, 8 * 128))
    pfirst_d = din("pfirst", (128, 512))
    out_d = nc.dram_tensor("out", [S, D], F32, kind="ExternalOutput").ap()

    sc = Sched(nc)

    def sb(name, shape, dt):
        return nc.alloc_sbuf_tensor("sb_" + name, list(shape), dt)

    def bufs(name, n):
        return [Buf(f"{name}{i}") for i in range(n)]

    hT = sb("hT", (128, KC, ST), F32)
    hTb = bufs("hT", KC)
    xnT = sb("xnT", (128, KC, ST), BF16)
    xnb = bufs("xn", KC)
    stage = [xnT[:, 4 * i:4 * i + 4, :].rearrange("p a b -> p (a b)").bitcast(F32) for i in range(2)]
    stageb = [xnb[0:4], xnb[4:8]]
    cstf = sb("cstf", (128, CF_N), F32)
    cstfb = Buf("cstf")
    cstb = sb("cstb", (128, CB_N), BF16)
    cstbb = Buf("cstb")
    ident = cstf[:, CF_IDENT:CF_IDENT + 128]
    onesf = cstf[:, CF_ONES:CF_ONES + 128]
    identh = cstf[:, CF_IDH:CF_IDH + 128]
    mcur = cstb[:, CB_MCUR:CB_MCUR + 128]
    mprev = cstb[:, CB_MPREV:CB_MPREV + 128]
    b64 = cstb[:, CB_B64:CB_B64 + 128]
    o1024 = cstb[:, CB_O1024:CB_O1024 + 128]
    o512 = cstb[:, CB_O512:CB_O512 + 128]
    opad = cstb[:, CB_OPAD:CB_OPAD + 256].rearrange("p (j c) -> p j c", j=2)
    identb = cstb[:, CB_IDB:CB_IDB + 128]
    mneg_cur = cstb[:, CB_NCUR:CB_NCUR + 128]
    mneg_prev = cstb[:, CB_NPREV:CB_NPREV + 128]
    pcur = cstb[:, CB_PCUR:CB_PCUR + 512].rearrange("p (g t) -> p g t", g=4)
    pprev = cstb[:, CB_PPREV:CB_PPREV + 512].rearrange("p (g t) -> p g t", g=4)
    cols = sb("cols", (128, KC, 128), F32)
    colsb = Buf("cols")
    esk = sb("esk", (128, 2, 4), F32)
    eskb = Buf("esk")
    KTpad = sb("KTpad", (128, 4, 4, NMEM), BF16)
    Vpad = sb("Vpad", (128, 4, 2, 4, 128), BF16)
    KTb = bufs("KT", 4)
    Vpb = bufs("Vp", 4)
    win = [sb(f"win{i}", (128, KC, 512), BF16) for i in range(3)]
    winb = [bufs(f"win{i}h", 2) for i in range(3)]
    wout = [sb(f"wout{i}", (128, 10, 256), BF16) for i in range(3)]
    woutb = bufs("wout", 3)
    diag = sb("diag", (128, 124, 128), BF16)
    diagb = Buf("diag")
    pw = sb("pw", (128, 4, 512), BF16)
    pwb = Buf("pw")
    wsT2 = [sb(f"wsT{i}", (128, 8, 128), BF16) for i in range(2)]
    wsTb2 = bufs("wsT", 2)
    BT2 = [sb(f"BT{i}", (128, 4, 128), F32) for i in range(2)]
    BTb2 = bufs("BT", 2)
    bw = pw[:, 0, :].rearrange("p (g d) -> p g d", g=4)
    bwb = pwb
    carryK = [sb(f"carryK{i}", (128, 4, 128), BF16) for i in range(2)]
    carryV = [sb(f"carryV{i}", (128, 4, 128), BF16) for i in range(2)]
    carryH = [sb(f"carryH{i}", (128, 4, 32), BF16) for i in range(2)]
    carryX = [sb(f"carryX{i}", (128, 512), BF16) for i in range(2)]
    cKb, cVb, cHb, cXb = bufs("cK", 2), bufs("cV", 2), bufs("cH", 2), bufs("cX", 2)
    sq = [sb(f"sq{i}", (128, ST), BF16) for i in range(4)]
    sqb = bufs("sq", 4)
    fs = [sb(f"fs{i}", (128, ST), F32) for i in range(4)]
    fsb = bufs("fs", 4)
    catT = sb("catT", (128, 10, ST), BF16)
    catb = bufs("cat", 10)
    sgA = sb("sgA", (128, 4, ST), BF16)
    sgAb = bufs("sgA", 4)
    sgB = sb("sgB", (128, 4, ST), BF16)
    sgBb = bufs("sgB", 4)
    sgM = sb("sgM", (128, 2, ST), BF16)
    sgMb = bufs("sgM", 2)
    qmn = sb("qmn", (128, 2, ST), BF16)
    qmnb = bufs("qmn", 2)
    PT = [sb(f"PT{i}", (128, 1024), BF16) for i in range(4)]
    PTb = bufs("PT", 4)
    diffT = [sb(f"diffT{i}", (128, ST), BF16) for i in range(4)]
    dfb = bufs("diff", 4)
    stt = [sb(f"stt{i}", (128, 16), F32) for i in range(2)]
    sttb = bufs("stt", 2)
    qnT = sb("qnT", (128, 4, ST), BF16)
    qnb = bufs("qn", 4)
    Kz = sb("Kz", (128, 4, 5 * 128), BF16)
    Kzb = bufs("Kz", 5)
    Vz = sb("Vz", (128, 5, 4, 128), BF16)
    Vzb = bufs("Vz", 5)
    xbT = Vz[:, :, :, :].rearrange("p s a c -> p s (a c)")
    xbTb = Vzb
    h1T = sb("h1T", (128, 4, 32 + ST), BF16)
    h1b = bufs("h1", 4)
    hn = sb("hn", (128, 4, ST), BF16)
    hnb = bufs("hn", 4)
    vnpad = [qnT[:, 0:2, :].rearrange("p a (h c) -> p (a h) c", c=128),
             qnT[:, 2:4, :].rearrange("p a (h c) -> p (a h) c", c=128),
             hn[:, 0:2, :].rearrange("p a (h c) -> p (a h) c", c=128),
             hn[:, 2:4, :].rearrange("p a (h c) -> p (a h) c", c=128)]
    vnb = [qnb[0:2], qnb[2:4], hnb[0:2], hnb[2:4]]
    stage.append(h1T[:, :, :].rearrange("p a b -> p (a b)")[:, 0:2048].bitcast(F32))
    stageb.append(h1b)
    stage.append(hn[:, :, :].rearrange("p a b -> p (a b)").bitcast(F32))
    stageb.append(hnb)
    x3_prefetched = [False]
    psum = [nc.alloc_psum_tensor(f"ps{i}", [128, 512], F32) for i in range(8)]
    psb = bufs("ps", 8)

    def MM(out, lhsT, rhs, start, stop, R, W, inc):
        sc.op("pe", lambda e: e.matmul(out, lhsT, rhs, start=start, stop=stop), R, W, inc)

    def TR(out, in_, idn, R, W, inc):
        sc.op("pe", lambda e: e.transpose(out=out, in_=in_, identity=idn), R, W, inc)

    def ACT(out, in_, func, R, W, bias=None, scale=None, accum_out=None):
        kw = {}
        if bias is not None:
            kw["bias"] = bias
        if scale is not None:
            kw["scale"] = scale
        if accum_out is not None:
            kw["accum_out"] = accum_out
        sc.op("act", lambda e: e.activation(out=out, in_=in_, func=func, **kw), R, W)

    def TT(out, in0, in1, op, R, W, eng="dve"):
        sc.op(eng, lambda e: e.tensor_tensor(out=out, in0=in0, in1=in1, op=op), R, W)

    def STT(out, in0, scalar, in1, op0, op1, R, W, eng="dve"):
        sc.op(eng, lambda e: e.scalar_tensor_tensor(out=out, in0=in0, scalar=scalar, in1=in1, op0=op0, op1=op1), R, W)

    def TS(out, in0, s1, s2, op0, op1, R, W, eng="dve"):
        if op1 is None:
            sc.op(eng, lambda e: e.tensor_scalar(out=out, in0=in0, scalar1=s1, scalar2=None, op0=op0), R, W)
        else:
            sc.op(eng, lambda e: e.tensor_scalar(out=out, in0=in0, scalar1=s1, scalar2=s2, op0=op0, op1=op1), R, W)

    def CP(out, in_, R, W, eng="dve"):
        if eng == "act":
            sc.op("act", lambda e: e.copy(out=out, in_=in_), R, W)
        else:
            sc.op(eng, lambda e: e.tensor_copy(out=out, in_=in_), R, W)

    def RCP(out, in_, R, W):
        sc.op("dve", lambda e: e.reciprocal(out=out, in_=in_), R, W)

    def MEMSET(ap, val, W, eng="dve"):
        sc.op(eng, lambda e: e.memset(ap, val), (), W)

    def DMA(queue, key, out, in_, R, W):
        return sc.dma(queue, key, lambda e: e.dma_start(out=out, in_=in_), R, W)

    def col(c, r):
        return cols[:, c, r:r + 1]

    def rsqrt_eps(out, in_, R, W):
        ACT(out, in_, AF.Ln, R, W, bias=EPS)
        ACT(out, out, AF.Exp, W, W, scale=-0.5)

    wseq = []
    for st in range(NST):
        for L in layers:
            gmap = EVEN_G if L % 2 == 0 else ODD_G
            order = (["m", "ga", "gb", "v", "u", "xb"] if L % 2 == 0
                     else ["db", "da", "kv", "q", "gc", "gd", "m"])
            for nm in order:
                wseq.append(("win", WIN_BASE[L] + gmap[nm]))
            for dp in range(4):
                wseq.append(("wout", L * 4 + dp))
    wptr = {"win": 0, "wout": 0}
    wlist = {"win": [w for w in wseq if w[0] == "win"], "wout": [w for w in wseq if w[0] == "wout"]}
    wemitted = {"win": 0, "wout": 0}
    bg = []

    def bg_flush(n):
        while bg and n > 0:
            fn, R, W = bg.pop(0)
            sc.op("pool", fn, R, W)
            n -= 1

    def w_emit(kind, upto):
        ring, rb, src = (win, winb, winp_d) if kind == "win" else (wout, woutb, woutp_d)
        n = len(ring)
        while wemitted[kind] < min(upto, len(wlist[kind])):
            i = wemitted[kind]
            gi = wlist[kind][i][1]
            slot = i % n
            if kind == "win":
                sv = src[gi].rearrange("p (k c) -> p k c", k=KC)
                for h_ in range(2):
                    DMA("pool", f"win{slot}h{h_}", ring[slot][:, :, h_ * 256:(h_ + 1) * 256],
                        sv[:, :, h_ * 256:(h_ + 1) * 256], (), [rb[slot][h_]])
            else:
                dst = ring[slot][:, :, :].rearrange("p a b -> p (a b)")
                DMA("pool", f"{kind}{slot}", dst, src[gi], (), [rb[slot]])
            wemitted[kind] += 1
            bg_flush(1)

    def prefetch_wout():
        w_emit("wout", wptr["wout"] + 3)

    def w_get(kind, gi, hold=0):
        i = wptr[kind]
        assert wlist[kind][i][1] == gi, (kind, i, wlist[kind][i], gi)
        n = len(win) if kind == "win" else len(wout)
        w_emit(kind, i + n - hold)
        wptr[kind] += 1
        slot = i % n
        return (win[slot], winb[slot]) if kind == "win" else (wout[slot], woutb[slot])

    def inproj_fm(wt, wbuf, chunks, banks, kouter=False):
        if kouter:
            for k in range(KC):
                for c, b in zip(chunks, banks):
                    MM(psum[b][:, :], wt[:, k, c * 128:(c + 1) * 128], xnT[:, k, :], k == 0, k == KC - 1,
                       [wbuf[c // 2], xnb[k]], [psb[b]], k == KC - 1)
            return
        for c, b in zip(chunks, banks):
            for k in range(KC):
                MM(psum[b][:, :], wt[:, k, c * 128:(c + 1) * 128], xnT[:, k, :], k == 0, k == KC - 1,
                   [wbuf[c // 2], xnb[k]], [psb[b]], k == KC - 1)

    rms_ready = [False]
    prefetch_hook = [None]

    def rms_norm_x(L):
        if not rms_ready[0]:
            for k in range(KC):
                ACT(sq[k % 2][:, :], hT[:, k, :], AF.Square, [hTb[k]], [sqb[k % 2]])
                MM(psum[7][:, :], o1024, sq[k % 2][:, :], k == 0, k == KC - 1, [cstbb, sqb[k % 2]], [psb[7]], True)
        rms_ready[0] = False
        rsqrt_eps(fs[0][:, :], psum[7][:, :], [psb[7]], [fsb[0]])
        for k in range(KC):
            STT(xnT[:, k, :], hT[:, k, :], col(k, R_NORM + L), fs[0][:, :], ALU.mult, ALU.mult,
                [hTb[k], colsb, fsb[0]], [xnb[k]])

    def out_proj(L, with_rms, border=(0, 1, 2, 3, 4, 5, 6, 0), korder=tuple(range(10))):
        pend = []

        def ss_mm():
            d_ = pend.pop(0)
            MM(psum[7][:, :], o1024, sq[d_ % 2][:, :], d_ == 0, d_ == KC - 1, [cstbb, sqb[d_ % 2]], [psb[7]], True)

        def finish(d, b):
            if pend:
                ss_mm()
            TT(hT[:, d, :], hT[:, d, :], psum[b][:, :], ALU.add, [psb[b], hTb[d]], [hTb[d]])
            if with_rms:
                ACT(sq[d % 2][:, :], hT[:, d, :], AF.Square, [hTb[d]], [sqb[d % 2]])
                pend.append(d)

        wts = [w_get("wout", L * 4 + dp, hold=dp) for dp in range(2)]
        for ki, k in enumerate(korder):
            for d in range(4):
                wt, wbuf = wts[d // 2]
                b = border[d]
                MM(psum[b][:, :], wt[:, k, (d % 2) * 128:(d % 2 + 1) * 128], catT[:, k, :], ki == 0, ki == 9,
                   [wbuf, catb[k]], [psb[b]], ki == 9)
        for d in range(4):
            finish(d, border[d])
        for dp in range(2, 4):
            wt, wbuf = w_get("wout", L * 4 + dp)
            for half in range(2):
                d = dp * 2 + half
                b = border[d]
                for k in range(10):
                    MM(psum[b][:, :], wt[:, k, half * 128:(half + 1) * 128], catT[:, k, :], k == 0, k == 9,
                       [wbuf, catb[k]], [psb[b]], k == 9)
                finish(d, b)
        while pend:
            ss_mm()
        rms_ready[0] = with_rms

    def head_norm(src_bank, ssbank, fsi, qcol_r, outs):
        si = ssbank % 2
        ACT(sq[si][:, :], psum[src_bank][:, :], AF.Square, [psb[src_bank]], [sqb[si]])
        MM(psum[ssbank][:, :], b64, sq[si][:, :], True, True, [cstbb, sqb[si]], [psb[ssbank]], True)
        rsqrt_eps(fs[fsi][:, :], psum[ssbank][:, :], [psb[ssbank]], [fsb[fsi]])
        for out_ap, psl, W in outs:
            STT(out_ap, psum[src_bank][psl, :], cols[psl, 0, qcol_r:qcol_r + 1], fs[fsi][psl, :],
                ALU.mult, ALU.mult, [psb[src_bank], colsb, fsb[fsi]], W)

    def head_norm_multi(items):
        for src, ssb, sqi, fsi, cr, outs in items:
            ACT(sq[sqi][:, :], psum[src][:, :], AF.Square, [psb[src]], [sqb[sqi]])
        for src, ssb, sqi, fsi, cr, outs in items:
            MM(psum[ssb][:, :], b64, sq[sqi][:, :], True, True, [cstbb, sqb[sqi]], [psb[ssb]], True)
        for src, ssb, sqi, fsi, cr, outs in items:
            ACT(fs[fsi][:, :], psum[ssb][:, :], AF.Ln, [psb[ssb]], [fsb[fsi]], bias=EPS)
        for src, ssb, sqi, fsi, cr, outs in items:
            ACT(fs[fsi][:, :], fs[fsi][:, :], AF.Exp, [fsb[fsi]], [fsb[fsi]], scale=-0.5)
        for src, ssb, sqi, fsi, cr, outs in items:
            for out_ap, psl, W in outs:
                STT(out_ap, psum[src][psl, :], cols[psl, 0, cr:cr + 1], fs[fsi][psl, :],
                    ALU.mult, ALU.mult, [psb[src], colsb, fsb[fsi]], W)

    def recip_act(out, in_, R, W):
        ACT(out, in_, AF.Ln, R, W)
        ACT(out, out, AF.Exp, W, W, scale=-1.0)

    def mem_head(L, qbanks, gbanks, phase=None, ssbanks=None):
        if ssbanks is None:
            ssbanks = gbanks
        if phase is None:
            for p in range(2):
                ACT(sgM[:, p, :], psum[gbanks[p]][:, :], AF.Silu, [psb[gbanks[p]]], [sgMb[p]])
        if phase in (None, 0):
            for p in range(2):
                ACT(sq[p][:, :], psum[qbanks[p]][:, :], AF.Square, [psb[qbanks[p]]], [sqb[p]])
        if phase == 2:
            for p in range(2):
                ACT(sgM[:, p, :], psum[gbanks[p]][:, :], AF.Silu, [psb[gbanks[p]]], [sgMb[p]])
        gbanks = ssbanks
        if phase in (None, 1):
            for p in range(2):
                MM(psum[gbanks[p]][:, :], b64, sq[p][:, :], True, True, [cstbb, sqb[p]], [psb[gbanks[p]]], True)
            for p in range(2):
                ACT(fs[2 + p][:, :], psum[gbanks[p]][:, :], AF.Ln, [psb[gbanks[p]]], [fsb[2 + p]], bias=EPS)
            for p in range(2):
                ACT(fs[2 + p][:, :], fs[2 + p][:, :], AF.Exp, [fsb[2 + p]], [fsb[2 + p]], scale=-0.5)
            for p in range(2):
                STT(qmn[:, p, :], psum[qbanks[p]][:, :], col(0, R_MQ + L), fs[2 + p][:, :], ALU.mult, ALU.mult,
                    [psb[qbanks[p]], colsb, fsb[2 + p]], [qmnb[p]])

    def mem_attn(L, qbanks, gbanks, sbanks, part=None):
        if part in (None, 1):
            mem_scores(L, sbanks)
        if part in (None, 2):
            mem_pv(L, qbanks, gbanks)

    def mem_scores(L, sbanks):
        for h in range(4):
            ptv = PT[h][:, :].rearrange("p (a b) -> p a b", a=2)
            for mh in range(2):
                b = sbanks[(h * 2 + mh) % 4]
                MM(psum[b][:, :], KTpad[:, L, h, mh * 128:(mh + 1) * 128], qmn[:, h // 2, :], True, True,
                   [KTb[L], qmnb[h // 2]], [psb[b]], True)
                ACT(ptv[:, mh, :], psum[b][:, :], AF.Exp, [psb[b]], [PTb[h]], scale=0.125)

    def mem_pv(L, qbanks, gbanks):
        for p in range(2):
            ob, db_ = gbanks[p], qbanks[p]
            n = 0
            for j in range(2):
                h = 2 * p + j
                ptv = PT[h][:, :].rearrange("p (a b) -> p a b", a=2)
                for mh in range(2):
                    MM(psum[ob][:, :], Vpad[:, L, mh, h, :], ptv[:, mh, :], n == 0, n == 3,
                       [Vpb[L], PTb[h]], [psb[ob]], n == 3)
                    n += 1
            n = 0
            for j in range(2):
                h = 2 * p + j
                ptv = PT[h][:, :].rearrange("p (a b) -> p a b", a=2)
                for mh in range(2):
                    MM(psum[db_][:, :], opad[:, j, :], ptv[:, mh, :], n == 0, n == 3,
                       [cstbb, PTb[h]], [psb[db_]], n == 3)
                    n += 1
            recip_act(fs[2][:, :], psum[db_][:, :], [psb[db_]], [fsb[2]])
            TT(fs[3][:, :], psum[ob][:, :], fs[2][:, :], ALU.mult, [psb[ob], fsb[2]], [fsb[3]])
            TT(catT[:, 8 + p, :], fs[3][:, :], sgM[:, p, :], ALU.mult, [fsb[3], sgMb[p]], [catb[8 + p]])

    def layer_even(L, st):
        i = L // 2
        first = (st == 0)
        base = WIN_BASE[L]
        wsT, wsTb, BT, BTb = wsT2[i], wsTb2[i], BT2[i], BTb2[i]
        DMA("pool", "pw", pw[:, 0, :], bwp_d[i], (), [bwb])
        rms_norm_x(L)
        for tb in range(4):
            MEMSET(vnpad[tb], 0.0, vnb[tb])
        if not first:
            CP(xbT[:, 0, :], carryX[i][:, :], [cXb[i]], [xbTb[0]])
        else:
            DMA("pool", "pfirst", PT[3][:, 0:512], pfirst_d, (), [PTb[3]])
        wt, wbuf = w_get("win", base + EVEN_G["m"])
        inproj_fm(wt, wbuf, range(4), [0, 1, 2, 3], kouter=True)
        mem_head(L, [0, 1], [2, 3], phase=0)
        wtg, wbufg = w_get("win", base + EVEN_G["ga"])
        inproj_fm(wtg, wbufg, [0], [4])
        mem_head(L, [0, 1], [2, 3], phase=1, ssbanks=[6, 7])
        inproj_fm(wtg, wbufg, [1, 2, 3], [5, 6, 7])
        mem_head(L, [0, 1], [2, 3], phase=2)
        for c in range(4):
            ACT(sgA[:, c, :], psum[4 + c][:, :], AF.Silu, [psb[4 + c]], [sgAb[c]])
        wt2, wbuf2 = w_get("win", base + EVEN_G["gb"])
        inproj_fm(wt2, wbuf2, range(4), [4, 5, 6, 7])
        for c in range(4):
            ACT(sgB[:, c, :], psum[4 + c][:, :], AF.Silu, [psb[4 + c]], [sgBb[c]])
        wt, wbuf = w_get("win", base + EVEN_G["v"])
        for tb in range(4):
            b = tb
            for k in range(KC):
                MM(psum[b][:, :], xnT[:, k, tb * 128:(tb + 1) * 128], wt[:, k, :], k == 0, k == KC - 1,
                   wbuf + [xnb[k]], [psb[b]], k == KC - 1)
            s_ = stt[tb % 2]
            sb_ = sttb[tb % 2]
            sc.op("dve", lambda e, s_=s_, b=b: e.bn_stats(out=s_[:, 0:6], in_=psum[b][:, :]), [psb[b]], [sb_])
            sc.op("dve", lambda e, s_=s_: e.bn_aggr(out=s_[:, 6:8], in_=s_[:, 0:6]), [sb_], [sb_])
            rsqrt_eps(s_[:, 8:9], s_[:, 7:8], [sb_], [sb_])
            STT(s_[:, 9:10], s_[:, 6:7], -1.0, s_[:, 8:9], ALU.mult, ALU.mult, [sb_], [sb_])
            vp = vnpad[tb].rearrange("p (q j) c -> p q j c", j=2)
            pv = psum[b][:, :].rearrange("p (q j d) -> p q j d", q=4, j=2)
            for j in range(2):
                ACT(vp[:, :, j, 64 * j:64 * j + 64], pv[:, :, j, :], AF.Identity, [psb[b], sb_], vnb[tb],
                    bias=s_[:, 9:10], scale=s_[:, 8:9])
        prefetch_wout()
        wtu, wbufu = w_get("win", base + EVEN_G["u"])
        inproj_fm(wtu, wbufu, range(4), [4, 5, 6, 7])
        for tb in range(4):
            for p in range(4):
                for j in range(2):
                    h = 2 * p + j
                    MM(psum[p][:, tb * 128:(tb + 1) * 128], vnpad[tb][:, h, :], wsT[:, h, :], j == 0, j == 1,
                       vnb[tb] + [wsTb], [psb[p]], j == 1)
        for p in range(4):
            t1 = fs[2 + (p % 2)]
            t1b = fsb[2 + (p % 2)]
            STT(t1[:, :].rearrange("p (a b) -> p a b", a=4), psum[p][:, :].rearrange("p (a b) -> p a b", a=4),
                col(p, R_ALNG + i), BT[:, p, :].unsqueeze(1).broadcast_to([128, 4, 128]), ALU.mult, ALU.add,
                [psb[p], colsb, BTb], [t1b])
            TT(t1[:, :], t1[:, :], sgA[:, p, :], ALU.mult, [t1b, sgAb[p]], [t1b])
            TT(catT[:, p, :], t1[:, :], psum[4 + p][:, :], ALU.mult, [t1b, psb[4 + p]], [catb[p]])
        wt, wbuf = w_get("win", base + EVEN_G["xb"])
        for tb in range(4):
            b = tb
            for k in range(KC):
                MM(psum[b][:, :], xnT[:, k, tb * 128:(tb + 1) * 128], wt[:, k, :], k == 0, k == KC - 1,
                   wbuf + [xnb[k]], [psb[b]], k == KC - 1)
            CP(xbT[:, 1 + tb, :], psum[b][:, :], [psb[b]], [xbTb[1 + tb]], eng="dve")
        CP(carryX[i][:, :], xbT[:, 4, :], [xbTb[4]], [cXb[i]])
        if not first:
            mem_attn(L, [4, 5], [6, 7], [0, 1, 2, 3], part=1)
        for g in range(4):
            for tb in range(4):
                o_ = psum[g][:, tb * 128:(tb + 1) * 128]
                if first and tb == 0:
                    MM(o_, xbT[:, 1, g * 128:(g + 1) * 128], PT[3][:, g * 128:(g + 1) * 128], True, True,
                       [xbTb[1], PTb[3]], [psb[g]], True)
                else:
                    MM(o_, xbT[:, tb, g * 128:(g + 1) * 128], pprev[:, g, :], True, False,
                       [xbTb[tb], cstbb], [psb[g]], False)
                    MM(o_, xbT[:, 1 + tb, g * 128:(g + 1) * 128], pcur[:, g, :], False, True,
                       [xbTb[1 + tb], cstbb], [psb[g]], True)
            CP(diffT[g][:, :], psum[g][:, :], [psb[g]], [dfb[g]], eng="act")
        for g in range(4):
            MM(psum[4 + g][:, :], bw[:, g, :], diffT[g][:, :], True, True, [bwb, dfb[g]], [psb[4 + g]], True)
            STT(catT[:, 4 + g, :], psum[4 + g][:, :], col(g, R_BSC + i), sgB[:, g, :], ALU.mult, ALU.mult,
                [psb[4 + g], colsb, sgBb[g]], [catb[4 + g]])
        if L == layers[-1] and st + 1 < NST:
            prefetch_hook[0](st + 1, (0, 1, 2))
        mem_attn(L, [4, 5], [6, 7], [0, 1, 2, 3], part=(None if first else 2))
        out_proj(L, L != layers[-1])

    def build_diag_now(i):
        r0 = R_DW + 32 * i
        for c in range(4):
            for j0 in range(0, 31, 8):
                nj = min(8, 31 - j0)
                bg.append((lambda e, c=c, j0=j0, nj=nj: e.tensor_tensor(
                    out=diag[:, c * 31 + j0:c * 31 + j0 + nj, :],
                    in0=identh.unsqueeze(1).broadcast_to([128, nj, 128]),
                    in1=cols[:, c, r0 + j0:r0 + j0 + nj].unsqueeze(2).broadcast_to([128, nj, 128]), op=ALU.mult),
                    [cstfb, colsb], [diagb]))

    odd_seq = [(st, L) for st in range(NST) for L in layers if L % 2 == 1]
    odd_pos = [0]

    def layer_odd(L, st):
        i = L // 2
        first = (st == 0)
        base = WIN_BASE[L]
        DMA("pool", "pw", pw[:, :, :].rearrange("p a b -> p (a b)"), pwp_d[i], (), [pwb])
        rms_norm_x(L)
        MEMSET(Vz[:, :, :, :], 0.0, Vzb)
        if not first:
            CP(Kz[:, :, 0:128], carryK[i][:, :, :], [cKb[i]], [Kzb[0]])
            CP(Vz[:, 0, :, :], carryV[i][:, :, :], [cVb[i]], [Vzb[0]])
        CP(h1T[:, :, 0:32], carryH[i][:, :, :], [cHb[i]], h1b)
        wt, wbuf = w_get("win", base + ODD_G["db"])
        inproj_fm(wt, wbuf, range(4), [0, 1, 2, 3], kouter=True)
        for c in range(4):
            ACT(hn[:, c, :], psum[c][:, :], AF.Tanh, [psb[c]], [hnb[c]], scale=0.5)
        wt, wbuf = w_get("win", base + ODD_G["da"])
        inproj_fm(wt, wbuf, range(4), [4, 5, 6, 7])
        for c in range(4):
            STT(h1T[:, c, 32:32 + ST], hn[:, c, :], 1.0, psum[4 + c][:, :], ALU.add, ALU.mult,
                [hnb[c], psb[4 + c]], [h1b[c]])
        CP(carryH[i][:, :, :], h1T[:, :, ST:ST + 32], h1b, [cHb[i]])
        bg_flush(10000)
        filler = []
        for c in range(4):
            for j in range(31):
                filler.append((c, j))

        def fill(n):
            while filler and n > 0:
                c, j = filler.pop(0)
                MM(psum[4 + c][:, :], diag[:, c * 31 + j, :], h1T[:, c, 2 + j:2 + j + ST], j == 0, j == 30,
                   [diagb, h1b[c]], [psb[4 + c]], j == 30)
                n -= 1

        wt, wbuf = w_get("win", base + ODD_G["kv"])
        inproj_fm(wt, wbuf, [0, 1], [0, 1])
        for tb in range(4):
            for k in range(KC):
                MM(psum[2][:, tb * 128:(tb + 1) * 128], xnT[:, k, tb * 128:(tb + 1) * 128], wt[:, k, 256:384],
                   k == 0, k == KC - 1, [wbuf[1], xnb[k]], [psb[2]], k == KC - 1)
        vzv = Vz[:, 1:5, :, :].rearrange("p s (kv j) c -> p s kv j c", j=2)
        pvv = psum[2][:, :].rearrange("p (t kv d) -> p t kv d", t=4, kv=2)
        for j in range(2):
            CP(vzv[:, :, :, j, 64 * j:64 * j + 64], pvv, [psb[2]], Vzb[1:5], eng="act")
        fill(10)
        items = []
        for kv in range(2):
            outs = []
            for j in range(2):
                psl = slice(64 * j, 64 * j + 64)
                outs.append((Kz[psl, kv * 2 + j, 128:640], psl, Kzb[1:5]))
            items.append((kv, 3 - kv, kv, kv, R_CK + i, outs))
        head_norm_multi(items)
        fill(12)
        wt, wbuf = w_get("win", base + ODD_G["q"])
        for qh in range(2):
            inproj_fm(wt, wbuf, [2 * qh, 2 * qh + 1], [0, 1])
            fill(6)
            head_norm_multi([(cc, 2 + cc, cc, cc, R_CQ + i,
                              [(qnT[:, 2 * qh + cc, :], slice(0, 128), [qnb[2 * qh + cc]])]) for cc in range(2)])
            fill(12)
        prefetch_wout()
        wt, wbuf = w_get("win", base + ODD_G["gc"])
        inproj_fm(wt, wbuf, range(4), [0, 1, 2, 3])
        for c in range(4):
            ACT(sgA[:, c, :], psum[c][:, :], AF.Silu, [psb[c]], [sgAb[c]])
        fill(4)
        wt, wbuf = w_get("win", base + ODD_G["gd"])
        inproj_fm(wt, wbuf, range(4), [0, 1, 2, 3])
        for c in range(4):
            ACT(sgB[:, c, :], psum[c][:, :], AF.Silu, [psb[c]], [sgBb[c]])
        fill(4)
        wt, wbuf = w_get("win", base + ODD_G["m"])
        inproj_fm(wt, wbuf, range(4), [0, 1, 2, 3])
        if L == layers[-1] and st + 1 < NST:
            prefetch_hook[0](st + 1, (1, 2))
        fill(3)
        mem_head(L, [0, 1], [2, 3])
        fill(12)
        nfill_blk = max(0, len(filler) // 4)
        for n in range(4):
            kbs = []
            if not (first and n == 0):
                kbs.append((n, mneg_prev))
            kbs.append((n + 1, mneg_cur))
            pts = []
            for ki, (slot, mneg) in enumerate(kbs):
                pti = (n * 2 + ki) % 4
                pts.append(pti)
                for half in range(2):
                    b = half
                    MM(psum[b][:, :].rearrange("p (h t) -> p h t", h=4), identb,
                       mneg.unsqueeze(1).broadcast_to([128, 4, 128]), True, False, [cstbb], [psb[b]], False)
                    for hh in range(4):
                        h = half * 4 + hh
                        MM(psum[b][:, hh * 128:(hh + 1) * 128],
                           Kz[:, (h // 4) * 2 + (h % 2), slot * 128:(slot + 1) * 128],
                           qnT[:, h // 2, n * 128:(n + 1) * 128], False, hh == 3,
                           [Kzb[slot], qnb[h // 2]], [psb[b]], hh == 3)
                    ACT(PT[pti][:, half * 512:(half + 1) * 512], psum[b][:, :], AF.Exp,
                        [psb[b]], [PTb[pti]], scale=0.125)
                fill(4)
            ob, db_ = 2, 3
            for which, bank in (("o", ob), ("d", db_)):
                for c in range(4):
                    nmm = 2 * len(kbs)
                    m = 0
                    for ki, (slot, mneg) in enumerate(kbs):
                        ptv = PT[pts[ki]][:, :].rearrange("p (h t) -> p h t", h=8)
                        for j in range(2):
                            if which == "o":
                                lhsT = Vz[:, slot, (c // 2) * 2 + j, :]
                                Rr = [Vzb[slot], PTb[pts[ki]]]
                            else:
                                lhsT = opad[:, j, :]
                                Rr = [cstbb, PTb[pts[ki]]]
                            MM(psum[bank][:, c * 128:(c + 1) * 128], lhsT, ptv[:, 2 * c + j, :], m == 0, m == nmm - 1,
                               Rr, [psb[bank]], (m == nmm - 1))
                            m += 1
            fill(nfill_blk - 4 * len(kbs))
            dv = psum[db_][:, :].rearrange("p (c t) -> p c t", c=4)
            TT(fs[0][:, :].rearrange("p (c t) -> p c t", c=4), dv,
               esk[:, i, :].unsqueeze(2).broadcast_to([128, 4, 128]), ALU.add, [psb[db_], eskb], [fsb[0]])
            recip_act(fs[1][:, :], fs[0][:, :], [fsb[0]], [fsb[1]])
            TT(fs[2][:, :], psum[ob][:, :], fs[1][:, :], ALU.mult, [psb[ob], fsb[1]], [fsb[2]])
            TT(catT[:, 0:4, n * 128:(n + 1) * 128], fs[2][:, :].rearrange("p (c t) -> p c t", c=4),
               sgA[:, :, n * 128:(n + 1) * 128], ALU.mult, [fsb[2]] + sgAb, catb[0:4])
        def ln_act(c):
            s0, s1 = 2 * (c % 2), 2 * (c % 2) + 1
            ACT(sq[s0][:, :], psum[4 + c][:, :], AF.Identity, [psb[4 + c], colsb], [sqb[s0]], bias=col(c, R_DWB + i))
            ACT(sq[s1][:, :], psum[4 + c][:, :], AF.Square, [psb[4 + c], colsb], [sqb[s1]], bias=col(c, R_DWB + i))

        def ln_mm(c):
            s0, s1 = 2 * (c % 2), 2 * (c % 2) + 1
            MM(psum[0][:, :], o512, sq[s0][:, :], c == 0, c == 3, [cstbb, sqb[s0]], [psb[0]], True)
            MM(psum[1][:, :], o512, sq[s1][:, :], c == 0, c == 3, [cstbb, sqb[s1]], [psb[1]], True)

        fill(max(0, len(filler) - 31))
        CP(carryK[i][:, :, :], Kz[:, :, 512:640], [Kzb[4]], [cKb[i]])
        CP(carryV[i][:, :, :], Vz[:, 4, :, :], [Vzb[4]], [cVb[i]])
        ln_act(0)
        ln_act(1)
        fill(16)
        ln_mm(0)
        ln_act(2)
        fill(1000)
        if L == layers[-1] and st + 1 < NST:
            prefetch_hook[0](st + 1, (0,))
        odd_pos[0] += 1
        if odd_pos[0] < len(odd_seq):
            build_diag_now(odd_seq[odd_pos[0]][1] // 2)
        ln_mm(1)
        ln_act(3)
        ln_mm(2)
        ln_mm(3)
        CP(fs[0][:, :], psum[0][:, :], [psb[0]], [fsb[0]])
        TT(fs[1][:, :], fs[0][:, :], fs[0][:, :], ALU.mult, [fsb[0]], [fsb[1]])
        TT(fs[1][:, :], psum[1][:, :], fs[1][:, :], ALU.subtract, [psb[1], fsb[1]], [fsb[1]])
        rsqrt_eps(fs[1][:, :], fs[1][:, :], [fsb[1]], [fsb[1]])
        mem_attn(L, [4, 5], [6, 7], [2, 3, 0, 1], part=1)
        for c in range(4):
            t = fs[2 + c % 2]
            tb_ = fsb[2 + c % 2]
            STT(t[:, :], psum[4 + c][:, :], col(c, R_DWB + i), fs[0][:, :], ALU.add, ALU.subtract,
                [psb[4 + c], colsb, fsb[0]], [tb_])
            TT(t[:, :], t[:, :], fs[1][:, :], ALU.mult, [tb_, fsb[1]], [tb_])
            ACT(hn[:, c, :], t[:, :], AF.Silu, [tb_, colsb], [hnb[c]], bias=col(c, R_DLNB + i), scale=col(c, R_DLNG + i))
        mem_attn(L, [2, 3], [0, 1], [2, 3, 0, 1], part=2)
        odd_border = (2, 3, 0, 1, 4, 5, 6, 2)
        for dc in range(4):
            for kc in range(4):
                MM(psum[4 + dc][:, :], pw[:, kc, dc * 128:(dc + 1) * 128], hn[:, kc, :], kc == 0, kc == 3,
                   [pwb, hnb[kc]], [psb[4 + dc]], kc == 3)
            TT(catT[:, 4 + dc, :], psum[4 + dc][:, :], sgB[:, dc, :], ALU.mult, [psb[4 + dc], sgBb[dc]], [catb[4 + dc]])
        if L == layers[-1] and st + 1 < NST:
            prefetch_hook[0](st + 1, (3,))
            x3_prefetched[0] = True
        out_proj(L, L != layers[-1], border=odd_border, korder=(0, 1, 2, 3, 8, 9, 4, 5, 6, 7))

    def x_dma(st, tt, si):
        tok = st * ST + tt * 128
        DMA("sp", f"stage{si}", stage[si], x_d[tok:tok + 128, :], (), stageb[si])

    hts = [hT[:, 2 * mt:2 * mt + 2, :].rearrange("p a b -> p (a b)") for mt in range(2)]
    htsb = [hTb[0:2], hTb[2:4]]
    PTf = [PT[h][:, :].bitcast(F32) for h in range(4)]
    DMA("sp", "cstf", cstf[:, :], cstf_d, (), [cstfb])
    DMA("pool", "cstb", cstb[:, :], cstb_d, (), [cstbb])
    DMA("sp", "stage0", stage[0], prow_d, (), stageb[0])
    for mt in range(2):
        DMA("sp", f"hts{mt}", hts[mt], mem_d[mt * 128:(mt + 1) * 128, :], (), htsb[mt])
    sgu_tmp = [([PTf[0], PTf[1]], [PTb[0], PTb[1]], [PTf[2], PTf[3]], [PTb[2], PTb[3]], "ptf"),
               ([fs[0][:, :], fs[1][:, :]], [fsb[0], fsb[1]], [fs[2][:, :], fs[3][:, :]], [fsb[2], fsb[3]], "fs")]
    for i in range(2):
        wr_, wrb_, ab_, abb_, kp = sgu_tmp[i]
        for hh in range(2):
            DMA("sp", f"{kp}{hh}", wr_[hh], wstp_d[i][:, hh * 512:(hh + 1) * 512], (), [wrb_[hh]])
            DMA("sp", f"{kp}{2 + hh}", ab_[hh], absb_d[i][:, hh * 512:(hh + 1) * 512], (), [abb_[hh]])
    x_dma(0, 0, 2)
    x_dma(0, 2, 1)
    x_dma(0, 3, 3)
    x3_prefetched[0] = True
    for L in range(3):
        DMA("pool", f"win{L}h0", win[L][:, :, :].rearrange("p a b -> p (a b)"), wkvp_d[L], (), winb[L])
    for ap_, b_ in ((Kz[:, :, :], Kzb), (KTpad[:, :, :, :], KTb), (Vpad[:, :, :, :, :], Vpb),
                    (carryH[0][:, :, :], [cHb[0]]), (carryH[1][:, :, :], [cHb[1]])):
        MEMSET(ap_, 0.0, list(b_))
    for half in range(2):
        for j in range(4):
            c = half * 4 + j
            TR(psum[half][:, j * 128:(j + 1) * 128], stage[0][:, c * 128:(c + 1) * 128], ident,
               stageb[0] + [cstfb], [psb[half]], j == 3)
        CP(cols[:, half * 4:(half + 1) * 4, :].rearrange("p a b -> p (a b)"), psum[half][:, :], [psb[half]], [colsb],
           eng=("act" if half == 0 else "dve"))
    x_dma(0, 1, 0)
    for i in range(2):
        ACT(esk[:, i, :], cols[:, 0:4, R_SINK + i], AF.Exp, [colsb], [eskb])
    for i in range(2):
        wr_, wrb_, ab_, abb_, kp = sgu_tmp[i]
        for hh in range(2):
            wv = wr_[hh].rearrange("p (h t) -> p h t", h=4)
            TT(wv, wv, mcur.unsqueeze(1).broadcast_to([128, 4, 128]), ALU.mult, [wrb_[hh], cstbb], [wrb_[hh]])
            MM(psum[4 + 2 * i + hh][:, :], onesf, wr_[hh], True, True, [cstfb, wrb_[hh]], [psb[4 + 2 * i + hh]], True)
            CP(wsT2[i][:, hh * 4:(hh + 1) * 4, :].rearrange("p a b -> p (a b)"), wr_[hh], [wrb_[hh]], [wsTb2[i]], eng="act")
        for p in range(4):
            for j in range(2):
                h = 2 * p + j
                hh, hl = divmod(h, 4)
                psl = slice(64 * j, 64 * j + 64)
                STT(BT2[i][psl, p, :], psum[4 + 2 * i + hh][psl, hl * 128:(hl + 1) * 128],
                    cols[psl, p, R_ALNB + i:R_ALNB + i + 1], ab_[hh][psl, hl * 128:(hl + 1) * 128],
                    ALU.mult, ALU.add, [psb[4 + 2 * i + hh], colsb, abb_[hh]], [BTb2[i]])
    memnT = catT[:, 0:4, :].rearrange("p a b -> p (a b)").rearrange("p (k m) -> p k m", k=KC)
    for mt in range(2):
        sg_, sgb_ = hts[mt], htsb[mt]
        ACT(fs[0][:, :] if mt == 0 else fs[2][:, :], sg_[:, 0:512], AF.Square, sgb_, [fsb[0] if mt == 0 else fsb[2]])
        ACT(fs[1][:, :] if mt == 0 else fs[3][:, :], sg_[:, 512:1024], AF.Square, sgb_, [fsb[1] if mt == 0 else fsb[3]])
        f0, f0b = (fs[0], fsb[0]) if mt == 0 else (fs[2], fsb[2])
        f1, f1b = (fs[1], fsb[1]) if mt == 0 else (fs[3], fsb[3])
        st_, stb_ = stt[mt], sttb[mt]
        TT(f0[:, :], f0[:, :], f1[:, :], ALU.add, [f0b, f1b], [f0b])
        sc.op("dve", lambda e, st_=st_, f0=f0: e.reduce_sum(out=st_[:, 2:3], in_=f0[:, :], axis=mybir.AxisListType.X),
              [f0b], [stb_])
        TS(st_[:, 2:3], st_[:, 2:3], 1.0 / D, None, ALU.mult, None, [stb_], [stb_])
        rsqrt_eps(st_[:, 3:4], st_[:, 2:3], [stb_], [stb_])
        TS(sg_, sg_, st_[:, 3:4], None, ALU.mult, None, sgb_ + [stb_], sgb_)
        for half in range(2):
            bk = 2 * mt + half
            for j in range(4):
                k = half * 4 + j
                TR(psum[bk][:, j * 128:(j + 1) * 128], sg_[:, k * 128:(k + 1) * 128], ident,
                   sgb_ + [cstfb], [psb[bk]], j == 3)
            for j in range(4):
                k = half * 4 + j
                TS(memnT[:, k, mt * 128:(mt + 1) * 128], psum[bk][:, j * 128:(j + 1) * 128], col(k, R_MEMG), None,
                   ALU.mult, None, [psb[bk], colsb], catb[0:4])
    for L in range(4):
        slot = L % 3
        wk, wkb = win[slot], winb[slot]
        if L == 3:
            DMA("pool", f"win{slot}h0", wk[:, :, :].rearrange("p a b -> p (a b)"), wkvp_d[L], (), wkb)
        for p in range(2):
            for k in range(KC):
                MM(psum[p][:, 0:NMEM], wk[:, k, p * 128:(p + 1) * 128], memnT[:, k, :], k == 0, k == KC - 1,
                   wkb + catb[0:4], [psb[p]], k == KC - 1)
            ACT(sq[p][:, 0:NMEM], psum[p][:, 0:NMEM], AF.Square, [psb[p]], [sqb[p]])
            MM(psum[4 + p][:, 0:NMEM], b64, sq[p][:, 0:NMEM], True, True, [cstbb, sqb[p]], [psb[4 + p]], True)
            rsqrt_eps(fs[p][:, 0:NMEM], psum[4 + p][:, 0:NMEM], [psb[4 + p]], [fsb[p]])
            for j in range(2):
                psl = slice(64 * j, 64 * j + 64)
                STT(KTpad[psl, L, 2 * p + j, :], psum[p][psl, 0:NMEM], cols[psl, 0, R_MK + L:R_MK + L + 1],
                    fs[p][psl, 0:NMEM], ALU.mult, ALU.mult, [psb[p], colsb, fsb[p]], [KTb[L]])
        for mh in range(2):
            b = 2 + mh
            for k in range(KC):
                MM(psum[b][:, 0:256], memnT[:, k, mh * 128:(mh + 1) * 128], wk[:, k, 256:512], k == 0, k == KC - 1,
                   wkb + catb[0:4], [psb[b]], k == KC - 1)
            vv = Vpad[:, L, mh, :, :].rearrange("p (q j) c -> p q j c", j=2)
            pvv = psum[b][:, 0:256].rearrange("p (q j d) -> p q j d", q=2, j=2)
            for j in range(2):
                CP(vv[:, :, j, 64 * j:64 * j + 64], pvv[:, :, j, :], [psb[b]], [Vpb[L]], eng=("act" if j == 0 else "dve"))
    if any(L % 2 == 1 for L in layers):
        build_diag_now(odd_seq[0][1] // 2)

    def prefetch_x(st, tiles=(0, 1, 2)):
        for tt in tiles:
            x_dma(st, tt, (2, 0, 1, 3)[tt])

    prefetch_hook[0] = prefetch_x

    for st in range(NST):
        for tt in range(4):
            si = (2, 0, 1, 3 if x3_prefetched[0] else 2)[tt]
            if tt == 3:
                if not x3_prefetched[0]:
                    x_dma(st, 3, 2)
                x3_prefetched[0] = False
            for half in range(2):
                b = (tt * 2 + half) % 8
                for j in range(4):
                    k = half * 4 + j
                    TR(psum[b][:, j * 128:(j + 1) * 128], stage[si][:, k * 128:(k + 1) * 128], ident,
                       stageb[si] + [cstfb], [psb[b]], j == 3)
                CP(hT[:, half * 4:(half + 1) * 4, tt * 128:(tt + 1) * 128],
                   psum[b][:, :].rearrange("p (j t) -> p j t", j=4), [psb[b]], hTb[half * 4:(half + 1) * 4],
                   eng=("act" if half == 0 else "dve"))
        for L in layers:
            if L % 2 == 0:
                layer_even(L, st)
            else:
                layer_odd(L, st)
        for tt in range(4):
            tok = st * ST + tt * 128
            for half in range(2):
                b = (tt * 2 + half) % 8
                hi = tt * 2 + half
                for j in range(4):
                    k = half * 4 + j
                    TR(psum[b][:, j * 128:(j + 1) * 128], hT[:, k, tt * 128:(tt + 1) * 128], ident,
                       [hTb[k], cstfb], [psb[b]], j == 3)
                if hi < 4:
                    sap, sbf, key = fs[hi][:, :], [fsb[hi]], f"fs{hi}"
                else:
                    c0 = 2 * (hi - 4)
                    sap = catT[:, c0:c0 + 2, :].rearrange("p a b -> p (a b)").bitcast(F32)
                    sbf, key = catb[c0:c0 + 2], f"cst{hi}"
                CP(sap, psum[b][:, :], [psb[b]], sbf, eng=("act" if half == 0 else "dve"))
                ev = DMA("sp", key, out_d[tok:tok + 128, half * 512:(half + 1) * 512], sap, sbf, ())
                sc.final.append(ev)

    last = {}
    for key, sem, val in sc.final:
        last[key] = (sem, val)
    sc.q["sp"].append((list(last.values()), lambda e: e.nop(), None))

    with nc.Block() as block:
        @block.sync
        def _(e):
            sc.emit("sp", e)

        @block.tensor
        def _(e):
            sc.emit("pe", e)

        @block.vector
        def _(e):
            sc.emit("dve", e)

        @block.scalar
        def _(e):
            sc.emit("act", e)

        @block.gpsimd
        def _(e):
            sc.emit("pool", e)
    return nc, sc


def _pack_inputs(inp):
    f = lambda a: np.ascontiguousarray(np.asarray(a, dtype=np.float32))
    P = {}
    cstf = np.zeros((128, CF_N), np.float32)
    cstf[:, CF_IDENT:CF_IDENT + 128] = np.eye(128)
    cstf[:, CF_ONES:CF_ONES + 128] = 1.0
    cstf[:, CF_IDH:CF_IDH + 128] = 0.5 * np.eye(128)
    cstb = np.zeros((128, CB_N), np.float32)
    s_ = np.arange(128)[:, None]
    t_ = np.arange(128)[None, :]
    cstb[:, CB_MCUR:CB_MCUR + 128] = (t_ >= s_)
    cstb[:, CB_MPREV:CB_MPREV + 128] = (s_ > t_)
    cstb[:, CB_B64:CB_B64 + 128] = ((s_ // 64) == (t_ // 64)) / 64.0
    cstb[:, CB_O1024:CB_O1024 + 128] = 1.0 / 1024.0
    cstb[:, CB_O512:CB_O512 + 128] = 1.0 / 512.0
    cstb[:, CB_OPAD:CB_OPAD + 64] = 1.0
    cstb[:, CB_OPAD + 128 + 64:CB_OPAD + 256] = 1.0
    cstb[:, CB_IDB:CB_IDB + 128] = np.eye(128)
    cstb[:, CB_NCUR:CB_NCUR + 128] = np.where(t_ >= s_, 0.0, -2400.0)
    cstb[:, CB_NPREV:CB_NPREV + 128] = np.where(s_ > t_, 0.0, -2400.0)
    pfirst = np.zeros((128, 512), np.float32)
    for g in range(4):
        w = 2 << g
        inwin = ((t_ - s_) >= 0) & ((t_ - s_) < w)
        cstb[:, CB_PCUR + g * 128:CB_PCUR + (g + 1) * 128] = np.where(inwin, 1.0 / w, 0.0) - (s_ == t_)
        cstb[:, CB_PPREV + g * 128:CB_PPREV + (g + 1) * 128] = np.where((t_ + 128 - s_) < w, 1.0 / w, 0.0)
        pfirst[:, g * 128:(g + 1) * 128] = np.where(inwin, 1.0 / np.minimum(t_ + 1, w), 0.0) - (s_ == t_)
    P["cstf"], P["cstb"], P["pfirst"] = cstf, cstb, pfirst
    prow = np.zeros((128, D), np.float32)
    prow[R_NORM:R_NORM + 4] = f(inp["norm_g"])
    prow[R_MEMG] = f(inp["mem_norm_g"])
    for i in range(2):
        prow[R_ALNG + i, :512] = f(inp["a_ln_g"])[i]
        prow[R_ALNB + i, :512] = f(inp["a_ln_b"])[i]
        prow[R_BSC + i, :512] = f(inp["b_scale"])[i]
        prow[R_DWB + i, :512] = f(inp["d_dw_b"])[i]
        prow[R_DLNG + i, :512] = f(inp["d_ln_g"])[i]
        prow[R_DLNB + i, :512] = f(inp["d_ln_b"])[i]
        prow[R_CQ + i, :128] = np.tile(f(inp["c_qnorm"])[i], 2)
        prow[R_CK + i, :128] = np.tile(f(inp["c_knorm"])[i], 2)
        prow[R_SINK + i, :512] = np.repeat(f(inp["c_sink"])[i], 64)
        prow[R_DW + 32 * i:R_DW + 32 * i + 31, :512] = f(inp["d_dw"])[i]
    for L in range(4):
        prow[R_MQ + L, :128] = np.tile(f(inp["m_qnorm"])[L], 2)
        prow[R_MK + L, :128] = np.tile(f(inp["m_knorm"])[L], 2)
    P["prow"] = prow

    def pk(w, colidx):
        sub = w[:, colidx].reshape(KC, 128, len(colidx)).transpose(1, 0, 2)
        return np.ascontiguousarray(sub).reshape(128, -1)

    winp = np.zeros((26, 128, KC * 512), np.float32)
    for L in range(4):
        i = L // 2
        if L % 2 == 0:
            w = f(inp["w_in_even"])[i]
            for g in range(6):
                winp[WIN_BASE[L] + g] = pk(w, np.arange(g * 512, (g + 1) * 512))
        else:
            w = f(inp["w_in_odd"])[i]
            segs = {"q": (0, 512), "gc": (768, 1280), "da": (1280, 1792), "db": (1792, 2304), "gd": (2304, 2816),
                    "m": (2816, 3328)}
            for nm, (a, b) in segs.items():
                winp[WIN_BASE[L] + ODD_G[nm]] = pk(w, np.arange(a, b))
            idx = np.concatenate([np.arange(512, 576), np.arange(512, 576), np.arange(576, 640), np.arange(576, 640),
                                  np.arange(640, 768)])
            tmp = np.zeros((128, KC, 512), np.float32)
            tmp[:, :, :384] = pk(w, idx).reshape(128, KC, 384)
            winp[WIN_BASE[L] + ODD_G["kv"]] = tmp.reshape(128, -1)
    P["winp"] = winp
    woutp = np.zeros((16, 128, 10 * 256), np.float32)
    for L in range(4):
        w = f(inp["w_out_even"] if L % 2 == 0 else inp["w_out_odd"])[L // 2]
        for dp in range(4):
            sub = w[:, dp * 256:(dp + 1) * 256].reshape(10, 128, 256).transpose(1, 0, 2)
            woutp[L * 4 + dp] = np.ascontiguousarray(sub).reshape(128, -1)
    P["woutp"] = woutp
    P["wkvp"] = np.stack([pk(f(inp["w_mem_kv"])[L], np.arange(512)) for L in range(4)])
    P["pwp"] = np.stack([np.ascontiguousarray(f(inp["d_pw"])[i].reshape(4, 128, 512).transpose(1, 0, 2)).reshape(128, -1)
                         for i in range(2)])
    P["bwp"] = np.stack([np.ascontiguousarray(f(inp["b_w"])[i].transpose(1, 0, 2)).reshape(128, -1) for i in range(2)])
    P["wstp"] = np.stack([np.ascontiguousarray(f(inp["a_ws"])[i].transpose(2, 0, 1)).reshape(128, -1) for i in range(2)])
    P["absb"] = np.stack([np.ascontiguousarray(np.broadcast_to(f(inp["a_bs"])[i].reshape(1, -1), (128, 1024)))
                          for i in range(2)])
    return P


_CACHE = {}


def run_layers(inputs, layers, trace=False):
    key = tuple(layers)
    if key not in _CACHE:
        _CACHE[key] = build_program(list(layers))
    nc, sc = _CACHE[key]
    x = np.ascontiguousarray(np.asarray(inputs["x"], dtype=np.float32))
    mem = np.ascontiguousarray(np.asarray(inputs["mem"], dtype=np.float32))
    P = _pack_inputs(inputs)
    in_maps = []
    for b in range(NCORES):
        m = {"x": x[b], "mem": mem[b]}
        m.update(P)
        in_maps.append(m)
    res = run_bass_kernel_spmd(nc, in_maps, core_ids=list(range(NCORES)), trace=trace)
    outp = np.stack([np.asarray(r["out"]) for r in res.results], axis=0)
    return outp.astype(np.float32), res


def kernel(**inputs):
    outp, _ = run_layers(inputs, (0, 1, 2, 3))
    return outp
```

```python
import numpy as np
import concourse.bass as bass
import concourse.mybir as mybir
from concourse.bass_utils import run_bass_kernel_spmd

F32 = mybir.dt.float32
BF16 = mybir.dt.bfloat16
ALU = mybir.AluOpType
AF = mybir.ActivationFunctionType

D = 1024
S = 4096
KC = 8
ST = 512
NST = S // ST
NCORES = 8
EPS = 1e-6
NMEM = 256

R_NORM, R_MEMG, R_ALNG, R_ALNB, R_BSC, R_DWB, R_DLNG, R_DLNB = 0, 4, 5, 7, 9, 11, 13, 15
R_MQ, R_MK, R_CQ, R_CK, R_SINK, R_DW = 17, 21, 25, 27, 29, 32
CF_IDENT, CF_ONES, CF_RC16, CF_IDH, CF_N = 0, 128, 256, 320, 448
CB_MCUR, CB_MPREV, CB_B64, CB_O1024, CB_O512, CB_OPAD = 0, 128, 256, 384, 512, 640
CB_IDB, CB_NCUR, CB_NPREV, CB_PCUR, CB_PPREV, CB_N = 896, 1024, 1152, 1280, 1792, 2304
EVEN_G = {"u": 0, "v": 1, "ga": 2, "xb": 3, "gb": 4, "m": 5}
ODD_G = {"q": 0, "gc": 1, "da": 2, "db": 3, "gd": 4, "m": 5, "kv": 6}
WIN_BASE = [0, 6, 13, 19]


class Buf:
    __slots__ = ("name", "w", "r")

    def __init__(self, name):
        self.name = name
        self.w = None
        self.r = []


class Sched:
    ENG = ("pe", "dve", "act", "pool", "sp")

    def __init__(self, nc):
        self.nc = nc
        self.q = {e: [] for e in self.ENG}
        self.sem = {e: nc.alloc_semaphore("c_" + e) for e in ("pe", "dve", "act", "pool")}
        self.cnt = {e: 0 for e in self.sem}
        self.known = {e: {} for e in self.ENG}
        self.dsem = {}
        self.dtot = {}
        self.final = []
        self.nins = 0

    def _deps(self, eng, R, W):
        need = {}

        def add(ev, raw):
            key, sem, val = ev
            if key == eng and eng == "pe":
                return
            if self.known[eng].get(key, 0) >= val:
                return
            if key not in need or need[key][1] < val:
                need[key] = (sem, val)

        for b in R:
            if b.w is not None:
                add(b.w, True)
        for b in W:
            if b.w is not None:
                add(b.w, False)
            for ev in b.r:
                add(ev, False)
        for key, (sem, val) in need.items():
            self.known[eng][key] = val
        return list(need.values())

    def _record(self, ev, R, W):
        for b in R:
            b.r.append(ev)
        for b in W:
            b.w = ev
            b.r = []

    def op(self, eng, fn, R=(), W=(), inc=True):
        waits = self._deps(eng, R, W)
        sem = self.sem[eng]
        if inc:
            self.cnt[eng] += 1
            val = self.cnt[eng]
        else:
            assert eng == "pe"
            val = self.cnt[eng] + 1
        self._record((eng, sem, val), R, W)
        self.q[eng].append((waits, fn, (sem, 1) if inc else None))
        self.nins += 1

    def dma(self, queue, key, fn, R=(), W=()):
        waits = self._deps(queue, R, W)
        if key not in self.dsem:
            self.dsem[key] = self.nc.alloc_semaphore("d_" + key)
            self.dtot[key] = 0
        self.dtot[key] += 16
        ev = (("d", key), self.dsem[key], self.dtot[key])
        self._record(ev, R, W)
        self.q[queue].append((waits, fn, (self.dsem[key], 16)))
        self.nins += 1
        return ev

    def emit(self, eng, e):
        for waits, fn, inc in self.q[eng]:
            for sem, val in waits:
                e.wait_ge(sem, val)
            ins = fn(e)
            if inc is not None:
                ins.then_inc(inc[0], inc[1])


def build_program(layers):
    nc = bass.Bass("TRN2", target_bir_lowering=False)

    def din(name, shape):
        return nc.dram_tensor(name, list(shape), F32, kind="ExternalInput").ap()

    x_d = din("x", (S, D))
    mem_d = din("mem", (NMEM, D))
    prow_d = din("prow", (128, D))
    cstf_d = din("cstf", (128, CF_N))
    cstb_d = din("cstb", (128, CB_N))
    winp_d = din("winp", (26, 128, KC * 512))
    woutp_d = din("woutp", (16, 128, 10 * 256))
    wkvp_d = din("wkvp", (4, 128, KC * 512))
    pwp_d = din("pwp", (2, 128, 4 * 512))
    bwp_d = din("bwp", (2, 128, 4 * 128))
    wstp_d = din("wstp", (2, 128, 8 * 128))
    absb_d = din("absb", (2, 128, 8 * 128))
    pfirst_d = din("pfirst", (128, 512))
    out_d = nc.dram_tensor("out", [S, D], F32, kind="ExternalOutput").ap()

    sc = Sched(nc)

    def sb(name, shape, dt):
        return nc.alloc_sbuf_tensor("sb_" + name, list(shape), dt)

    def bufs(name, n):
        return [Buf(f"{name}{i}") for i in range(n)]

    hT = sb("hT", (128, KC, ST), F32)
    hTb = bufs("hT", KC)
    xnT = sb("xnT", (128, KC, ST), BF16)
    xnb = bufs("xn", KC)
    stage = [xnT[:, 4 * i:4 * i + 4, :].rearrange("p a b -> p (a b)").bitcast(F32) for i in range(2)]
    stageb = [xnb[0:4], xnb[4:8]]
    cstf = sb("cstf", (128, CF_N), F32)
    cstfb = Buf("cstf")
    cstb = sb("cstb", (128, CB_N), BF16)
    cstbb = Buf("cstb")
    ident = cstf[:, CF_IDENT:CF_IDENT + 128]
    onesf = cstf[:, CF_ONES:CF_ONES + 128]
    identh = cstf[:, CF_IDH:CF_IDH + 128]
    mcur = cstb[:, CB_MCUR:CB_MCUR + 128]
    mprev = cstb[:, CB_MPREV:CB_MPREV + 128]
    b64 = cstb[:, CB_B64:CB_B64 + 128]
    o1024 = cstb[:, CB_O1024:CB_O1024 + 128]
    o512 = cstb[:, CB_O512:CB_O512 + 128]
    opad = cstb[:, CB_OPAD:CB_OPAD + 256].rearrange("p (j c) -> p j c", j=2)
    identb = cstb[:, CB_IDB:CB_IDB + 128]
    mneg_cur = cstb[:, CB_NCUR:CB_NCUR + 128]
    mneg_prev = cstb[:, CB_NPREV:CB_NPREV + 128]
    pcur = cstb[:, CB_PCUR:CB_PCUR + 512].rearrange("p (g t) -> p g t", g=4)
    pprev = cstb[:, CB_PPREV:CB_PPREV + 512].rearrange("p (g t) -> p g t", g=4)
    cols = sb("cols", (128, KC, 128), F32)
    colsb = Buf("cols")
    esk = sb("esk", (128, 2, 4), F32)
    eskb = Buf("esk")
    KTpad = sb("KTpad", (128, 4, 4, NMEM), BF16)
    Vpad = sb("Vpad", (128, 4, 2, 4, 128), BF16)
    KTb = bufs("KT", 4)
    Vpb = bufs("Vp", 4)
    win = [sb(f"win{i}", (128, KC, 512), BF16) for i in range(3)]
    winb = [bufs(f"win{i}h", 2) for i in range(3)]
    wout = [sb(f"wout{i}", (128, 10, 256), BF16) for i in range(3)]
    woutb = bufs("wout", 3)
    diag = sb("diag", (128, 124, 128), BF16)
    diagb = Buf("diag")
    pw = sb("pw", (128, 4, 512), BF16)
    pwb = Buf("pw")
    wsT2 = [sb(f"wsT{i}", (128, 8, 128), BF16) for i in range(2)]
    wsTb2 = bufs("wsT", 2)
    BT2 = [sb(f"BT{i}", (128, 4, 128), F32) for i in range(2)]
    BTb2 = bufs("BT", 2)
    bw = pw[:, 0, :].rearrange("p (g d) -> p g d", g=4)
    bwb = pwb
    carryK = [sb(f"carryK{i}", (128, 4, 128), BF16) for i in range(2)]
    carryV = [sb(f"carryV{i}", (128, 4, 128), BF16) for i in range(2)]
    carryH = [sb(f"carryH{i}", (128, 4, 32), BF16) for i in range(2)]
    carryX = [sb(f"carryX{i}", (128, 512), BF16) for i in range(2)]
    cKb, cVb, cHb, cXb = bufs("cK", 2), bufs("cV", 2), bufs("cH", 2), bufs("cX", 2)
    sq = [sb(f"sq{i}", (128, ST), BF16) for i in range(4)]
    sqb = bufs("sq", 4)
    fs = [sb(f"fs{i}", (128, ST), F32) for i in range(4)]
    fsb = bufs("fs", 4)
    catT = sb("catT", (128, 10, ST), BF16)
    catb = bufs("cat", 10)
    sgA = sb("sgA", (128, 4, ST), BF16)
    sgAb = bufs("sgA", 4)
    sgB = sb("sgB", (128, 4, ST), BF16)
    sgBb = bufs("sgB", 4)
    sgM = sb("sgM", (128, 2, ST), BF16)
    sgMb = bufs("sgM", 2)
    qmn = sb("qmn", (128, 2, ST), BF16)
    qmnb = bufs("qmn", 2)
    PT = [sb(f"PT{i}", (128, 1024), BF16) for i in range(4)]
    PTb = bufs("PT", 4)
    diffT = [sb(f"diffT{i}", (128, ST), BF16) for i in range(4)]
    dfb = bufs("diff", 4)
    stt = [sb(f"stt{i}", (128, 16), F32) for i in range(2)]
    sttb = bufs("stt", 2)
    qnT = sb("qnT", (128, 4, ST), BF16)
    qnb = bufs("qn", 4)
    Kz = sb("Kz", (128, 4, 5 * 128), BF16)
    Kzb = bufs("Kz", 5)
    Vz = sb("Vz", (128, 5, 4, 128), BF16)
    Vzb = bufs("Vz", 5)
    xbT = Vz[:, :, :, :].rearrange("p s a c -> p s (a c)")
    xbTb = Vzb
    h1T = sb("h1T", (128, 4, 32 + ST), BF16)
    h1b = bufs("h1", 4)
    hn = sb("hn", (128, 4, ST), BF16)
    hnb = bufs("hn", 4)
    vnpad = [qnT[:, 0:2, :].rearrange("p a (h c) -> p (a h) c", c=128),
             qnT[:, 2:4, :].rearrange("p a (h c) -> p (a h) c", c=128),
             hn[:, 0:2, :].rearrange("p a (h c) -> p (a h) c", c=128),
             hn[:, 2:4, :].rearrange("p a (h c) -> p (a h) c", c=128)]
    vnb = [qnb[0:2], qnb[2:4], hnb[0:2], hnb[2:4]]
    stage.append(h1T[:, :, :].rearrange("p a b -> p (a b)")[:, 0:2048].bitcast(F32))
    stageb.append(h1b)
    stage.append(hn[:, :, :].rearrange("p a b -> p (a b)").bitcast(F32))
    stageb.append(hnb)
    x3_prefetched = [False]
    psum = [nc.alloc_psum_tensor(f"ps{i}", [128, 512], F32) for i in range(8)]
    psb = bufs("ps", 8)

    def MM(out, lhsT, rhs, start, stop, R, W, inc):
        sc.op("pe", lambda e: e.matmul(out, lhsT, rhs, start=start, stop=stop), R, W, inc)

    def TR(out, in_, idn, R, W, inc):
        sc.op("pe", lambda e: e.transpose(out=out, in_=in_, identity=idn), R, W, inc)

    def ACT(out, in_, func, R, W, bias=None, scale=None, accum_out=None):
        kw = {}
        if bias is not None:
            kw["bias"] = bias
        if scale is not None:
            kw["scale"] = scale
        if accum_out is not None:
            kw["accum_out"] = accum_out
        sc.op("act", lambda e: e.activation(out=out, in_=in_, func=func, **kw), R, W)

    def TT(out, in0, in1, op, R, W, eng="dve"):
        sc.op(eng, lambda e: e.tensor_tensor(out=out, in0=in0, in1=in1, op=op), R, W)

    def STT(out, in0, scalar, in1, op0, op1, R, W, eng="dve"):
        sc.op(eng, lambda e: e.scalar_tensor_tensor(out=out, in0=in0, scalar=scalar, in1=in1, op0=op0, op1=op1), R, W)

    def TS(out, in0, s1, s2, op0, op1, R, W, eng="dve"):
        if op1 is None:
            sc.op(eng, lambda e: e.tensor_scalar(out=out, in0=in0, scalar1=s1, scalar2=None, op0=op0), R, W)
        else:
            sc.op(eng, lambda e: e.tensor_scalar(out=out, in0=in0, scalar1=s1, scalar2=s2, op0=op0, op1=op1), R, W)

    def CP(out, in_, R, W, eng="dve"):
        if eng == "act":
            sc.op("act", lambda e: e.copy(out=out, in_=in_), R, W)
        else:
            sc.op(eng, lambda e: e.tensor_copy(out=out, in_=in_), R, W)

    def RCP(out, in_, R, W):
        sc.op("dve", lambda e: e.reciprocal(out=out, in_=in_), R, W)

    def MEMSET(ap, val, W, eng="dve"):
        sc.op(eng, lambda e: e.memset(ap, val), (), W)

    def DMA(queue, key, out, in_, R, W):
        return sc.dma(queue, key, lambda e: e.dma_start(out=out, in_=in_), R, W)

    def col(c, r):
        return cols[:, c, r:r + 1]

    def rsqrt_eps(out, in_, R, W):
        ACT(out, in_, AF.Ln, R, W, bias=EPS)
        ACT(out, out, AF.Exp, W, W, scale=-0.5)

    wseq = []
    for st in range(NST):
        for L in layers:
            gmap = EVEN_G if L % 2 == 0 else ODD_G
            order = (["m", "ga", "gb", "v", "u", "xb"] if L % 2 == 0
                     else ["db", "da", "gc", "gd", "m", "kv", "q"])
            for nm in order:
                wseq.append(("win", WIN_BASE[L] + gmap[nm]))
            for dp in range(4):
                wseq.append(("wout", L * 4 + dp))
    wptr = {"win": 0, "wout": 0}
    wlist = {"win": [w for w in wseq if w[0] == "win"], "wout": [w for w in wseq if w[0] == "wout"]}
    wemitted = {"win": 0, "wout": 0}
    bg = []

    def bg_flush(n):
        while bg and n > 0:
            fn, R, W = bg.pop(0)
            sc.op("pool", fn, R, W)
            n -= 1

    def w_emit(kind, upto):
        ring, rb, src = (win, winb, winp_d) if kind == "win" else (wout, woutb, woutp_d)
        n = len(ring)
        while wemitted[kind] < min(upto, len(wlist[kind])):
            i = wemitted[kind]
            gi = wlist[kind][i][1]
            slot = i % n
            if kind == "win":
                sv = src[gi].rearrange("p (k c) -> p k c", k=KC)
                for h_ in range(2):
                    DMA("pool", f"win{slot}h{h_}", ring[slot][:, :, h_ * 256:(h_ + 1) * 256],
                        sv[:, :, h_ * 256:(h_ + 1) * 256], (), [rb[slot][h_]])
            else:
                dst = ring[slot][:, :, :].rearrange("p a b -> p (a b)")
                DMA("pool", f"{kind}{slot}", dst, src[gi], (), [rb[slot]])
            wemitted[kind] += 1
            bg_flush(1)

    def prefetch_wout():
        w_emit("wout", wptr["wout"] + 3)

    def w_get(kind, gi, hold=0):
        i = wptr[kind]
        assert wlist[kind][i][1] == gi, (kind, i, wlist[kind][i], gi)
        n = len(win) if kind == "win" else len(wout)
        w_emit(kind, i + n - hold)
        wptr[kind] += 1
        slot = i % n
        return (win[slot], winb[slot]) if kind == "win" else (wout[slot], woutb[slot])

    def inproj_fm(wt, wbuf, chunks, banks, kouter=False):
        if kouter:
            for k in range(KC):
                for c, b in zip(chunks, banks):
                    MM(psum[b][:, :], wt[:, k, c * 128:(c + 1) * 128], xnT[:, k, :], k == 0, k == KC - 1,
                       [wbuf[c // 2], xnb[k]], [psb[b]], k == KC - 1)
            return
        for c, b in zip(chunks, banks):
            for k in range(KC):
                MM(psum[b][:, :], wt[:, k, c * 128:(c + 1) * 128], xnT[:, k, :], k == 0, k == KC - 1,
                   [wbuf[c // 2], xnb[k]], [psb[b]], k == KC - 1)

    rms_ready = [False]
    prefetch_hook = [None]

    def rms_norm_x(L):
        if not rms_ready[0]:
            for k in range(KC):
                ACT(sq[k % 2][:, :], hT[:, k, :], AF.Square, [hTb[k]], [sqb[k % 2]])
                MM(psum[7][:, :], o1024, sq[k % 2][:, :], k == 0, k == KC - 1, [cstbb, sqb[k % 2]], [psb[7]], True)
        rms_ready[0] = False
        rsqrt_eps(fs[0][:, :], psum[7][:, :], [psb[7]], [fsb[0]])
        for k in range(KC):
            STT(xnT[:, k, :], hT[:, k, :], col(k, R_NORM + L), fs[0][:, :], ALU.mult, ALU.mult,
                [hTb[k], colsb, fsb[0]], [xnb[k]])

    def out_proj(L, with_rms, border=(0, 1, 2, 3, 4, 5, 6, 0), korder=tuple(range(10))):
        pend = []

        def ss_mm():
            d_ = pend.pop(0)
            MM(psum[7][:, :], o1024, sq[d_ % 2][:, :], d_ == 0, d_ == KC - 1, [cstbb, sqb[d_ % 2]], [psb[7]], True)

        def finish(d, b):
            if pend:
                ss_mm()
            TT(hT[:, d, :], hT[:, d, :], psum[b][:, :], ALU.add, [psb[b], hTb[d]], [hTb[d]])
            if with_rms:
                ACT(sq[d % 2][:, :], hT[:, d, :], AF.Square, [hTb[d]], [sqb[d % 2]])
                pend.append(d)

        wts = [w_get("wout", L * 4 + dp, hold=dp) for dp in range(2)]
        for ki, k in enumerate(korder):
            for d in range(4):
                wt, wbuf = wts[d // 2]
                b = border[d]
                MM(psum[b][:, :], wt[:, k, (d % 2) * 128:(d % 2 + 1) * 128], catT[:, k, :], ki == 0, ki == 9,
                   [wbuf, catb[k]], [psb[b]], ki == 9)
        for d in range(4):
            finish(d, border[d])
        for dp in range(2, 4):
            wt, wbuf = w_get("wout", L * 4 + dp)
            for half in range(2):
                d = dp * 2 + half
                b = border[d]
                for k in range(10):
                    MM(psum[b][:, :], wt[:, k, half * 128:(half + 1) * 128], catT[:, k, :], k == 0, k == 9,
                       [wbuf, catb[k]], [psb[b]], k == 9)
                finish(d, b)
        while pend:
            ss_mm()
        rms_ready[0] = with_rms

    def head_norm(src_bank, ssbank, fsi, qcol_r, outs):
        si = ssbank % 2
        ACT(sq[si][:, :], psum[src_bank][:, :], AF.Square, [psb[src_bank]], [sqb[si]])
        MM(psum[ssbank][:, :], b64, sq[si][:, :], True, True, [cstbb, sqb[si]], [psb[ssbank]], True)
        rsqrt_eps(fs[fsi][:, :], psum[ssbank][:, :], [psb[ssbank]], [fsb[fsi]])
        for out_ap, psl, W in outs:
            STT(out_ap, psum[src_bank][psl, :], cols[psl, 0, qcol_r:qcol_r + 1], fs[fsi][psl, :],
                ALU.mult, ALU.mult, [psb[src_bank], colsb, fsb[fsi]], W)

    def head_norm_multi(items):
        for src, ssb, sqi, fsi, cr, outs in items:
            ACT(sq[sqi][:, :], psum[src][:, :], AF.Square, [psb[src]], [sqb[sqi]])
        for src, ssb, sqi, fsi, cr, outs in items:
            MM(psum[ssb][:, :], b64, sq[sqi][:, :], True, True, [cstbb, sqb[sqi]], [psb[ssb]], True)
        for src, ssb, sqi, fsi, cr, outs in items:
            ACT(fs[fsi][:, :], psum[ssb][:, :], AF.Ln, [psb[ssb]], [fsb[fsi]], bias=EPS)
        for src, ssb, sqi, fsi, cr, outs in items:
            ACT(fs[fsi][:, :], fs[fsi][:, :], AF.Exp, [fsb[fsi]], [fsb[fsi]], scale=-0.5)
        for src, ssb, sqi, fsi, cr, outs in items:
            for out_ap, psl, W in outs:
                STT(out_ap, psum[src][psl, :], cols[psl, 0, cr:cr + 1], fs[fsi][psl, :],
                    ALU.mult, ALU.mult, [psb[src], colsb, fsb[fsi]], W)

    def recip_act(out, in_, R, W):
        ACT(out, in_, AF.Ln, R, W)
        ACT(out, out, AF.Exp, W, W, scale=-1.0)

    def mem_head(L, qbanks, gbanks, phase=None, ssbanks=None):
        if ssbanks is None:
            ssbanks = gbanks
        if phase is None:
            for p in range(2):
                ACT(sgM[:, p, :], psum[gbanks[p]][:, :], AF.Silu, [psb[gbanks[p]]], [sgMb[p]])
        if phase in (None, 0):
            for p in range(2):
                ACT(sq[p][:, :], psum[qbanks[p]][:, :], AF.Square, [psb[qbanks[p]]], [sqb[p]])
        if phase == 2:
            for p in range(2):
                ACT(sgM[:, p, :], psum[gbanks[p]][:, :], AF.Silu, [psb[gbanks[p]]], [sgMb[p]])
        gbanks = ssbanks
        if phase in (None, 1):
            for p in range(2):
                MM(psum[gbanks[p]][:, :], b64, sq[p][:, :], True, True, [cstbb, sqb[p]], [psb[gbanks[p]]], True)
            for p in range(2):
                ACT(fs[2 + p][:, :], psum[gbanks[p]][:, :], AF.Ln, [psb[gbanks[p]]], [fsb[2 + p]], bias=EPS)
            for p in range(2):
                ACT(fs[2 + p][:, :], fs[2 + p][:, :], AF.Exp, [fsb[2 + p]], [fsb[2 + p]], scale=-0.5)
            for p in range(2):
                STT(qmn[:, p, :], psum[qbanks[p]][:, :], col(0, R_MQ + L), fs[2 + p][:, :], ALU.mult, ALU.mult,
                    [psb[qbanks[p]], colsb, fsb[2 + p]], [qmnb[p]])

    def mem_attn(L, qbanks, gbanks, sbanks, part=None):
        if part in (None, 1):
            mem_scores(L, sbanks)
        if part in (None, 2):
            mem_pv(L, qbanks, gbanks)

    def mem_scores(L, sbanks):
        for h in range(4):
            ptv = PT[h][:, :].rearrange("p (a b) -> p a b", a=2)
            for mh in range(2):
                b = sbanks[(h * 2 + mh) % 4]
                MM(psum[b][:, :], KTpad[:, L, h, mh * 128:(mh + 1) * 128], qmn[:, h // 2, :], True, True,
                   [KTb[L], qmnb[h // 2]], [psb[b]], True)
                ACT(ptv[:, mh, :], psum[b][:, :], AF.Exp, [psb[b]], [PTb[h]], scale=0.125)

    def mem_pv(L, qbanks, gbanks):
        for p in range(2):
            ob, db_ = gbanks[p], qbanks[p]
            n = 0
            for j in range(2):
                h = 2 * p + j
                ptv = PT[h][:, :].rearrange("p (a b) -> p a b", a=2)
                for mh in range(2):
                    MM(psum[ob][:, :], Vpad[:, L, mh, h, :], ptv[:, mh, :], n == 0, n == 3,
                       [Vpb[L], PTb[h]], [psb[ob]], n == 3)
                    n += 1
            n = 0
            for j in range(2):
                h = 2 * p + j
                ptv = PT[h][:, :].rearrange("p (a b) -> p a b", a=2)
                for mh in range(2):
                    MM(psum[db_][:, :], opad[:, j, :], ptv[:, mh, :], n == 0, n == 3,
                       [cstbb, PTb[h]], [psb[db_]], n == 3)
                    n += 1
            recip_act(fs[2][:, :], psum[db_][:, :], [psb[db_]], [fsb[2]])
            TT(fs[3][:, :], psum[ob][:, :], fs[2][:, :], ALU.mult, [psb[ob], fsb[2]], [fsb[3]])
            TT(catT[:, 8 + p, :], fs[3][:, :], sgM[:, p, :], ALU.mult, [fsb[3], sgMb[p]], [catb[8 + p]])

    def layer_even(L, st):
        i = L // 2
        first = (st == 0)
        base = WIN_BASE[L]
        wsT, wsTb, BT, BTb = wsT2[i], wsTb2[i], BT2[i], BTb2[i]
        DMA("pool", "pw", pw[:, 0, :], bwp_d[i], (), [bwb])
        rms_norm_x(L)
        for tb in range(4):
            MEMSET(vnpad[tb], 0.0, vnb[tb])
        if not first:
            CP(xbT[:, 0, :], carryX[i][:, :], [cXb[i]], [xbTb[0]])
        else:
            DMA("pool", "pfirst", PT[3][:, 0:512], pfirst_d, (), [PTb[3]])
        wt, wbuf = w_get("win", base + EVEN_G["m"])
        inproj_fm(wt, wbuf, range(4), [0, 1, 2, 3], kouter=True)
        mem_head(L, [0, 1], [2, 3], phase=0)
        wtg, wbufg = w_get("win", base + EVEN_G["ga"])
        inproj_fm(wtg, wbufg, [0], [4])
        mem_head(L, [0, 1], [2, 3], phase=1, ssbanks=[6, 7])
        inproj_fm(wtg, wbufg, [1, 2, 3], [5, 6, 7])
        mem_head(L, [0, 1], [2, 3], phase=2)
        for c in range(4):
            ACT(sgA[:, c, :], psum[4 + c][:, :], AF.Silu, [psb[4 + c]], [sgAb[c]])
        wt2, wbuf2 = w_get("win", base + EVEN_G["gb"])
        inproj_fm(wt2, wbuf2, range(4), [4, 5, 6, 7])
        for c in range(4):
            ACT(sgB[:, c, :], psum[4 + c][:, :], AF.Silu, [psb[4 + c]], [sgBb[c]])
        wt, wbuf = w_get("win", base + EVEN_G["v"])
        for tb in range(4):
            b = tb
            for k in range(KC):
                MM(psum[b][:, :], xnT[:, k, tb * 128:(tb + 1) * 128], wt[:, k, :], k == 0, k == KC - 1,
                   wbuf + [xnb[k]], [psb[b]], k == KC - 1)
            s_ = stt[tb % 2]
            sb_ = sttb[tb % 2]
            sc.op("dve", lambda e, s_=s_, b=b: e.bn_stats(out=s_[:, 0:6], in_=psum[b][:, :]), [psb[b]], [sb_])
            sc.op("dve", lambda e, s_=s_: e.bn_aggr(out=s_[:, 6:8], in_=s_[:, 0:6]), [sb_], [sb_])
            rsqrt_eps(s_[:, 8:9], s_[:, 7:8], [sb_], [sb_])
            STT(s_[:, 9:10], s_[:, 6:7], -1.0, s_[:, 8:9], ALU.mult, ALU.mult, [sb_], [sb_])
            vp = vnpad[tb].rearrange("p (q j) c -> p q j c", j=2)
            pv = psum[b][:, :].rearrange("p (q j d) -> p q j d", q=4, j=2)
            for j in range(2):
                ACT(vp[:, :, j, 64 * j:64 * j + 64], pv[:, :, j, :], AF.Identity, [psb[b], sb_], vnb[tb],
                    bias=s_[:, 9:10], scale=s_[:, 8:9])
        prefetch_wout()
        wtu, wbufu = w_get("win", base + EVEN_G["u"])
        inproj_fm(wtu, wbufu, range(4), [4, 5, 6, 7])
        for tb in range(4):
            for p in range(4):
                for j in range(2):
                    h = 2 * p + j
                    MM(psum[p][:, tb * 128:(tb + 1) * 128], vnpad[tb][:, h, :], wsT[:, h, :], j == 0, j == 1,
                       vnb[tb] + [wsTb], [psb[p]], j == 1)
        for p in range(4):
            t1 = fs[2 + (p % 2)]
            t1b = fsb[2 + (p % 2)]
            STT(t1[:, :].rearrange("p (a b) -> p a b", a=4), psum[p][:, :].rearrange("p (a b) -> p a b", a=4),
                col(p, R_ALNG + i), BT[:, p, :].unsqueeze(1).broadcast_to([128, 4, 128]), ALU.mult, ALU.add,
                [psb[p], colsb, BTb], [t1b])
            TT(t1[:, :], t1[:, :], sgA[:, p, :], ALU.mult, [t1b, sgAb[p]], [t1b])
            TT(catT[:, p, :], t1[:, :], psum[4 + p][:, :], ALU.mult, [t1b, psb[4 + p]], [catb[p]])
        wt, wbuf = w_get("win", base + EVEN_G["xb"])
        for tb in range(4):
            b = tb
            for k in range(KC):
                MM(psum[b][:, :], xnT[:, k, tb * 128:(tb + 1) * 128], wt[:, k, :], k == 0, k == KC - 1,
                   wbuf + [xnb[k]], [psb[b]], k == KC - 1)
            CP(xbT[:, 1 + tb, :], psum[b][:, :], [psb[b]], [xbTb[1 + tb]], eng="dve")
        CP(carryX[i][:, :], xbT[:, 4, :], [xbTb[4]], [cXb[i]])
        if not first:
            mem_attn(L, [4, 5], [6, 7], [0, 1, 2, 3], part=1)
        for g in range(4):
            for tb in range(4):
                o_ = psum[g][:, tb * 128:(tb + 1) * 128]
                if first and tb == 0:
                    MM(o_, xbT[:, 1, g * 128:(g + 1) * 128], PT[3][:, g * 128:(g + 1) * 128], True, True,
                       [xbTb[1], PTb[3]], [psb[g]], True)
                else:
                    MM(o_, xbT[:, tb, g * 128:(g + 1) * 128], pprev[:, g, :], True, False,
                       [xbTb[tb], cstbb], [psb[g]], False)
                    MM(o_, xbT[:, 1 + tb, g * 128:(g + 1) * 128], pcur[:, g, :], False, True,
                       [xbTb[1 + tb], cstbb], [psb[g]], True)
            CP(diffT[g][:, :], psum[g][:, :], [psb[g]], [dfb[g]], eng="act")
        for g in range(4):
            MM(psum[4 + g][:, :], bw[:, g, :], diffT[g][:, :], True, True, [bwb, dfb[g]], [psb[4 + g]], True)
            STT(catT[:, 4 + g, :], psum[4 + g][:, :], col(g, R_BSC + i), sgB[:, g, :], ALU.mult, ALU.mult,
                [psb[4 + g], colsb, sgBb[g]], [catb[4 + g]])
        if L == layers[-1] and st + 1 < NST:
            prefetch_hook[0](st + 1, (0, 1, 2))
        mem_attn(L, [4, 5], [6, 7], [0, 1, 2, 3], part=(None if first else 2))
        out_proj(L, L != layers[-1])

    def build_diag_now(i):
        r0 = R_DW + 32 * i
        for c in range(4):
            for j0 in range(0, 31, 8):
                nj = min(8, 31 - j0)
                bg.append((lambda e, c=c, j0=j0, nj=nj: e.tensor_tensor(
                    out=diag[:, c * 31 + j0:c * 31 + j0 + nj, :],
                    in0=identh.unsqueeze(1).broadcast_to([128, nj, 128]),
                    in1=cols[:, c, r0 + j0:r0 + j0 + nj].unsqueeze(2).broadcast_to([128, nj, 128]), op=ALU.mult),
                    [cstfb, colsb], [diagb]))

    odd_seq = [(st, L) for st in range(NST) for L in layers if L % 2 == 1]
    odd_pos = [0]

    def layer_odd(L, st):
        i = L // 2
        first = (st == 0)
        base = WIN_BASE[L]
        DMA("pool", "pw", pw[:, :, :].rearrange("p a b -> p (a b)"), pwp_d[i], (), [pwb])
        rms_norm_x(L)
        MEMSET(Vz[:, :, :, :], 0.0, Vzb)
        if not first:
            CP(Kz[:, :, 0:128], carryK[i][:, :, :], [cKb[i]], [Kzb[0]])
            CP(Vz[:, 0, :, :], carryV[i][:, :, :], [cVb[i]], [Vzb[0]])
        CP(h1T[:, :, 0:32], carryH[i][:, :, :], [cHb[i]], h1b)
        wt, wbuf = w_get("win", base + ODD_G["db"])
        inproj_fm(wt, wbuf, range(4), [0, 1, 2, 3], kouter=True)
        for c in range(4):
            ACT(hn[:, c, :], psum[c][:, :], AF.Tanh, [psb[c]], [hnb[c]], scale=0.5)
        wt, wbuf = w_get("win", base + ODD_G["da"])
        inproj_fm(wt, wbuf, range(4), [4, 5, 6, 7])
        for c in range(4):
            STT(h1T[:, c, 32:32 + ST], hn[:, c, :], 1.0, psum[4 + c][:, :], ALU.add, ALU.mult,
                [hnb[c], psb[4 + c]], [h1b[c]])
        CP(carryH[i][:, :, :], h1T[:, :, ST:ST + 32], h1b, [cHb[i]])
        bg_flush(10000)
        filler = []
        for c in range(4):
            for j in range(31):
                filler.append((c, j))

        def fill(n):
            while filler and n > 0:
                c, j = filler.pop(0)
                MM(psum[4 + c][:, :], diag[:, c * 31 + j, :], h1T[:, c, 2 + j:2 + j + ST], j == 0, j == 30,
                   [diagb, h1b[c]], [psb[4 + c]], j == 30)
                n -= 1

        prefetch_wout()
        wt, wbuf = w_get("win", base + ODD_G["gc"])
        inproj_fm(wt, wbuf, range(4), [0, 1, 2, 3])
        for c in range(4):
            ACT(sgA[:, c, :], psum[c][:, :], AF.Silu, [psb[c]], [sgAb[c]])
        fill(4)
        wt, wbuf = w_get("win", base + ODD_G["gd"])
        inproj_fm(wt, wbuf, range(4), [0, 1, 2, 3])
        for c in range(4):
            ACT(sgB[:, c, :], psum[c][:, :], AF.Silu, [psb[c]], [sgBb[c]])
        fill(4)
        wt, wbuf = w_get("win", base + ODD_G["m"])
        inproj_fm(wt, wbuf, range(4), [0, 1, 2, 3])
        fill(3)
        mem_head(L, [0, 1], [2, 3])
        fill(12)
        wt, wbuf = w_get("win", base + ODD_G["kv"])
        inproj_fm(wt, wbuf, [0, 1], [0, 1])
        for tb in range(4):
            for k in range(KC):
                MM(psum[2][:, tb * 128:(tb + 1) * 128], xnT[:, k, tb * 128:(tb + 1) * 128], wt[:, k, 256:384],
                   k == 0, k == KC - 1, [wbuf[1], xnb[k]], [psb[2]], k == KC - 1)
        vzv = Vz[:, 1:5, :, :].rearrange("p s (kv j) c -> p s kv j c", j=2)
        pvv = psum[2][:, :].rearrange("p (t kv d) -> p t kv d", t=4, kv=2)
        for j in range(2):
            CP(vzv[:, :, :, j, 64 * j:64 * j + 64], pvv, [psb[2]], Vzb[1:5], eng="act")
        fill(10)
        items = []
        for kv in range(2):
            outs = []
            for j in range(2):
                psl = slice(64 * j, 64 * j + 64)
                outs.append((Kz[psl, kv * 2 + j, 128:640], psl, Kzb[1:5]))
            items.append((kv, 3 - kv, kv, kv, R_CK + i, outs))
        head_norm_multi(items)
        fill(12)
        wt, wbuf = w_get("win", base + ODD_G["q"])
        for qh in range(2):
            inproj_fm(wt, wbuf, [2 * qh, 2 * qh + 1], [0, 1])
            fill(6)
            head_norm_multi([(cc, 2 + cc, cc, cc, R_CQ + i,
                              [(qnT[:, 2 * qh + cc, :], slice(0, 128), [qnb[2 * qh + cc]])]) for cc in range(2)])
            fill(12)
        if L == layers[-1] and st + 1 < NST:
            prefetch_hook[0](st + 1, (1, 2))
        nfill_blk = max(0, len(filler) // 4)
        for n in range(4):
            kbs = []
            if not (first and n == 0):
                kbs.append((n, mneg_prev))
            kbs.append((n + 1, mneg_cur))
            pts = []
            for ki, (slot, mneg) in enumerate(kbs):
                pti = (n * 2 + ki) % 4
                pts.append(pti)
                for half in range(2):
                    b = half
                    MM(psum[b][:, :].rearrange("p (h t) -> p h t", h=4), identb,
                       mneg.unsqueeze(1).broadcast_to([128, 4, 128]), True, False, [cstbb], [psb[b]], False)
                    for hh in range(4):
                        h = half * 4 + hh
                        MM(psum[b][:, hh * 128:(hh + 1) * 128],
                           Kz[:, (h // 4) * 2 + (h % 2), slot * 128:(slot + 1) * 128],
                           qnT[:, h // 2, n * 128:(n + 1) * 128], False, hh == 3,
                           [Kzb[slot], qnb[h // 2]], [psb[b]], hh == 3)
                    ACT(PT[pti][:, half * 512:(half + 1) * 512], psum[b][:, :], AF.Exp,
                        [psb[b]], [PTb[pti]], scale=0.125)
                fill(4)
            ob, db_ = 2, 3
            for which, bank in (("o", ob), ("d", db_)):
                for c in range(4):
                    nmm = 2 * len(kbs)
                    m = 0
                    for ki, (slot, mneg) in enumerate(kbs):
                        ptv = PT[pts[ki]][:, :].rearrange("p (h t) -> p h t", h=8)
                        for j in range(2):
                            if which == "o":
                                lhsT = Vz[:, slot, (c // 2) * 2 + j, :]
                                Rr = [Vzb[slot], PTb[pts[ki]]]
                            else:
                                lhsT = opad[:, j, :]
                                Rr = [cstbb, PTb[pts[ki]]]
                            MM(psum[bank][:, c * 128:(c + 1) * 128], lhsT, ptv[:, 2 * c + j, :], m == 0, m == nmm - 1,
                               Rr, [psb[bank]], (m == nmm - 1))
                            m += 1
            fill(nfill_blk - 4 * len(kbs))
            dv = psum[db_][:, :].rearrange("p (c t) -> p c t", c=4)
            TT(fs[0][:, :].rearrange("p (c t) -> p c t", c=4), dv,
               esk[:, i, :].unsqueeze(2).broadcast_to([128, 4, 128]), ALU.add, [psb[db_], eskb], [fsb[0]])
            recip_act(fs[1][:, :], fs[0][:, :], [fsb[0]], [fsb[1]])
            TT(fs[2][:, :], psum[ob][:, :], fs[1][:, :], ALU.mult, [psb[ob], fsb[1]], [fsb[2]])
            TT(catT[:, 0:4, n * 128:(n + 1) * 128], fs[2][:, :].rearrange("p (c t) -> p c t", c=4),
               sgA[:, :, n * 128:(n + 1) * 128], ALU.mult, [fsb[2]] + sgAb, catb[0:4])
        def ln_act(c):
            s0, s1 = 2 * (c % 2), 2 * (c % 2) + 1
            ACT(sq[s0][:, :], psum[4 + c][:, :], AF.Identity, [psb[4 + c], colsb], [sqb[s0]], bias=col(c, R_DWB + i))
            ACT(sq[s1][:, :], psum[4 + c][:, :], AF.Square, [psb[4 + c], colsb], [sqb[s1]], bias=col(c, R_DWB + i))

        def ln_mm(c):
            s0, s1 = 2 * (c % 2), 2 * (c % 2) + 1
            MM(psum[0][:, :], o512, sq[s0][:, :], c == 0, c == 3, [cstbb, sqb[s0]], [psb[0]], True)
            MM(psum[1][:, :], o512, sq[s1][:, :], c == 0, c == 3, [cstbb, sqb[s1]], [psb[1]], True)

        fill(max(0, len(filler) - 31))
        CP(carryK[i][:, :, :], Kz[:, :, 512:640], [Kzb[4]], [cKb[i]])
        CP(carryV[i][:, :, :], Vz[:, 4, :, :], [Vzb[4]], [cVb[i]])
        ln_act(0)
        ln_act(1)
        fill(16)
        ln_mm(0)
        ln_act(2)
        fill(1000)
        if L == layers[-1] and st + 1 < NST:
            prefetch_hook[0](st + 1, (0,))
        odd_pos[0] += 1
        if odd_pos[0] < len(odd_seq):
            build_diag_now(odd_seq[odd_pos[0]][1] // 2)
        ln_mm(1)
        ln_act(3)
        ln_mm(2)
        ln_mm(3)
        CP(fs[0][:, :], psum[0][:, :], [psb[0]], [fsb[0]])
        TT(fs[1][:, :], fs[0][:, :], fs[0][:, :], ALU.mult, [fsb[0]], [fsb[1]])
        TT(fs[1][:, :], psum[1][:, :], fs[1][:, :], ALU.subtract, [psb[1], fsb[1]], [fsb[1]])
        rsqrt_eps(fs[1][:, :], fs[1][:, :], [fsb[1]], [fsb[1]])
        mem_attn(L, [4, 5], [6, 7], [2, 3, 0, 1], part=1)
        for c in range(4):
            t = fs[2 + c % 2]
            tb_ = fsb[2 + c % 2]
            STT(t[:, :], psum[4 + c][:, :], col(c, R_DWB + i), fs[0][:, :], ALU.add, ALU.subtract,
                [psb[4 + c], colsb, fsb[0]], [tb_])
            TT(t[:, :], t[:, :], fs[1][:, :], ALU.mult, [tb_, fsb[1]], [tb_])
            ACT(hn[:, c, :], t[:, :], AF.Silu, [tb_, colsb], [hnb[c]], bias=col(c, R_DLNB + i), scale=col(c, R_DLNG + i))
        mem_attn(L, [2, 3], [0, 1], [2, 3, 0, 1], part=2)
        odd_border = (2, 3, 0, 1, 4, 5, 6, 2)
        for dc in range(4):
            for kc in range(4):
                MM(psum[4 + dc][:, :], pw[:, kc, dc * 128:(dc + 1) * 128], hn[:, kc, :], kc == 0, kc == 3,
                   [pwb, hnb[kc]], [psb[4 + dc]], kc == 3)
            TT(catT[:, 4 + dc, :], psum[4 + dc][:, :], sgB[:, dc, :], ALU.mult, [psb[4 + dc], sgBb[dc]], [catb[4 + dc]])
        if L == layers[-1] and st + 1 < NST:
            prefetch_hook[0](st + 1, (3,))
            x3_prefetched[0] = True
        out_proj(L, L != layers[-1], border=odd_border, korder=(0, 1, 2, 3, 8, 9, 4, 5, 6, 7))

    def x_dma(st, tt, si):
        tok = st * ST + tt * 128
        DMA("sp", f"stage{si}", stage[si], x_d[tok:tok + 128, :], (), stageb[si])

    hts = [hT[:, 2 * mt:2 * mt + 2, :].rearrange("p a b -> p (a b)") for mt in range(2)]
    htsb = [hTb[0:2], hTb[2:4]]
    PTf = [PT[h][:, :].bitcast(F32) for h in range(4)]
    DMA("sp", "cstf", cstf[:, :], cstf_d, (), [cstfb])
    DMA("pool", "cstb", cstb[:, :], cstb_d, (), [cstbb])
    DMA("sp", "stage0", stage[0], prow_d, (), stageb[0])
    for mt in range(2):
        DMA("sp", f"hts{mt}", hts[mt], mem_d[mt * 128:(mt + 1) * 128, :], (), htsb[mt])
    sgu_tmp = [([PTf[0], PTf[1]], [PTb[0], PTb[1]], [PTf[2], PTf[3]], [PTb[2], PTb[3]], "ptf"),
               ([fs[0][:, :], fs[1][:, :]], [fsb[0], fsb[1]], [fs[2][:, :], fs[3][:, :]], [fsb[2], fsb[3]], "fs")]
    for i in range(2):
        wr_, wrb_, ab_, abb_, kp = sgu_tmp[i]
        for hh in range(2):
            DMA("sp", f"{kp}{hh}", wr_[hh], wstp_d[i][:, hh * 512:(hh + 1) * 512], (), [wrb_[hh]])
            DMA("sp", f"{kp}{2 + hh}", ab_[hh], absb_d[i][:, hh * 512:(hh + 1) * 512], (), [abb_[hh]])
    x_dma(0, 0, 2)
    x_dma(0, 2, 1)
    x_dma(0, 3, 3)
    x3_prefetched[0] = True
    for L in range(3):
        DMA("pool", f"win{L}h0", win[L][:, :, :].rearrange("p a b -> p (a b)"), wkvp_d[L], (), winb[L])
    for ap_, b_ in ((Kz[:, :, :], Kzb), (KTpad[:, :, :, :], KTb), (Vpad[:, :, :, :, :], Vpb),
                    (carryH[0][:, :, :], [cHb[0]]), (carryH[1][:, :, :], [cHb[1]])):
        MEMSET(ap_, 0.0, list(b_))
    for half in range(2):
        for j in range(4):
            c = half * 4 + j
            TR(psum[half][:, j * 128:(j + 1) * 128], stage[0][:, c * 128:(c + 1) * 128], ident,
               stageb[0] + [cstfb], [psb[half]], j == 3)
        CP(cols[:, half * 4:(half + 1) * 4, :].rearrange("p a b -> p (a b)"), psum[half][:, :], [psb[half]], [colsb],
           eng=("act" if half == 0 else "dve"))
    x_dma(0, 1, 0)
    for i in range(2):
        ACT(esk[:, i, :], cols[:, 0:4, R_SINK + i], AF.Exp, [colsb], [eskb])
    for i in range(2):
        wr_, wrb_, ab_, abb_, kp = sgu_tmp[i]
        for hh in range(2):
            wv = wr_[hh].rearrange("p (h t) -> p h t", h=4)
            TT(wv, wv, mcur.unsqueeze(1).broadcast_to([128, 4, 128]), ALU.mult, [wrb_[hh], cstbb], [wrb_[hh]])
            MM(psum[4 + 2 * i + hh][:, :], onesf, wr_[hh], True, True, [cstfb, wrb_[hh]], [psb[4 + 2 * i + hh]], True)
            CP(wsT2[i][:, hh * 4:(hh + 1) * 4, :].rearrange("p a b -> p (a b)"), wr_[hh], [wrb_[hh]], [wsTb2[i]], eng="act")
        for p in range(4):
            for j in range(2):
                h = 2 * p + j
                hh, hl = divmod(h, 4)
                psl = slice(64 * j, 64 * j + 64)
                STT(BT2[i][psl, p, :], psum[4 + 2 * i + hh][psl, hl * 128:(hl + 1) * 128],
                    cols[psl, p, R_ALNB + i:R_ALNB + i + 1], ab_[hh][psl, hl * 128:(hl + 1) * 128],
                    ALU.mult, ALU.add, [psb[4 + 2 * i + hh], colsb, abb_[hh]], [BTb2[i]])
    memnT = catT[:, 0:4, :].rearrange("p a b -> p (a b)").rearrange("p (k m) -> p k m", k=KC)
    for mt in range(2):
        sg_, sgb_ = hts[mt], htsb[mt]
        ACT(fs[0][:, :] if mt == 0 else fs[2][:, :], sg_[:, 0:512], AF.Square, sgb_, [fsb[0] if mt == 0 else fsb[2]])
        ACT(fs[1][:, :] if mt == 0 else fs[3][:, :], sg_[:, 512:1024], AF.Square, sgb_, [fsb[1] if mt == 0 else fsb[3]])
        f0, f0b = (fs[0], fsb[0]) if mt == 0 else (fs[2], fsb[2])
        f1, f1b = (fs[1], fsb[1]) if mt == 0 else (fs[3], fsb[3])
        st_, stb_ = stt[mt], sttb[mt]
        TT(f0[:, :], f0[:, :], f1[:, :], ALU.add, [f0b, f1b], [f0b])
        sc.op("dve", lambda e, st_=st_, f0=f0: e.reduce_sum(out=st_[:, 2:3], in_=f0[:, :], axis=mybir.AxisListType.X),
              [f0b], [stb_])
        TS(st_[:, 2:3], st_[:, 2:3], 1.0 / D, None, ALU.mult, None, [stb_], [stb_])
        rsqrt_eps(st_[:, 3:4], st_[:, 2:3], [stb_], [stb_])
        TS(sg_, sg_, st_[:, 3:4], None, ALU.mult, None, sgb_ + [stb_], sgb_)
        for half in range(2):
            bk = 2 * mt + half
            for j in range(4):
                k = half * 4 + j
                TR(psum[bk][:, j * 128:(j + 1) * 128], sg_[:, k * 128:(k + 1) * 128], ident,
                   sgb_ + [cstfb], [psb[bk]], j == 3)
            for j in range(4):
                k = half * 4 + j
                TS(memnT[:, k, mt * 128:(mt + 1) * 128], psum[bk][:, j * 128:(j + 1) * 128], col(k, R_MEMG), None,
                   ALU.mult, None, [psb[bk], colsb], catb[0:4])
    for L in range(4):
        slot = L % 3
        wk, wkb = win[slot], winb[slot]
        if L == 3:
            DMA("pool", f"win{slot}h0", wk[:, :, :].rearrange("p a b -> p (a b)"), wkvp_d[L], (), wkb)
        for p in range(2):
            for k in range(KC):
                MM(psum[p][:, 0:NMEM], wk[:, k, p * 128:(p + 1) * 128], memnT[:, k, :], k == 0, k == KC - 1,
                   wkb + catb[0:4], [psb[p]], k == KC - 1)
            ACT(sq[p][:, 0:NMEM], psum[p][:, 0:NMEM], AF.Square, [psb[p]], [sqb[p]])
            MM(psum[4 + p][:, 0:NMEM], b64, sq[p][:, 0:NMEM], True, True, [cstbb, sqb[p]], [psb[4 + p]], True)
            rsqrt_eps(fs[p][:, 0:NMEM], psum[4 + p][:, 0:NMEM], [psb[4 + p]], [fsb[p]])
            for j in range(2):
                psl = slice(64 * j, 64 * j + 64)
                STT(KTpad[psl, L, 2 * p + j, :], psum[p][psl, 0:NMEM], cols[psl, 0, R_MK + L:R_MK + L + 1],
                    fs[p][psl, 0:NMEM], ALU.mult, ALU.mult, [psb[p], colsb, fsb[p]], [KTb[L]])
        for mh in range(2):
            b = 2 + mh
            for k in range(KC):
                MM(psum[b][:, 0:256], memnT[:, k, mh * 128:(mh + 1) * 128], wk[:, k, 256:512], k == 0, k == KC - 1,
                   wkb + catb[0:4], [psb[b]], k == KC - 1)
            vv = Vpad[:, L, mh, :, :].rearrange("p (q j) c -> p q j c", j=2)
            pvv = psum[b][:, 0:256].rearrange("p (q j d) -> p q j d", q=2, j=2)
            for j in range(2):
                CP(vv[:, :, j, 64 * j:64 * j + 64], pvv[:, :, j, :], [psb[b]], [Vpb[L]], eng=("act" if j == 0 else "dve"))
    if any(L % 2 == 1 for L in layers):
        build_diag_now(odd_seq[0][1] // 2)

    def prefetch_x(st, tiles=(0, 1, 2)):
        for tt in tiles:
            x_dma(st, tt, (2, 0, 1, 3)[tt])

    prefetch_hook[0] = prefetch_x

    for st in range(NST):
        for tt in range(4):
            si = (2, 0, 1, 3 if x3_prefetched[0] else 2)[tt]
            if tt == 3:
                if not x3_prefetched[0]:
                    x_dma(st, 3, 2)
                x3_prefetched[0] = False
            for half in range(2):
                b = (tt * 2 + half) % 8
                for j in range(4):
                    k = half * 4 + j
                    TR(psum[b][:, j * 128:(j + 1) * 128], stage[si][:, k * 128:(k + 1) * 128], ident,
                       stageb[si] + [cstfb], [psb[b]], j == 3)
                CP(hT[:, half * 4:(half + 1) * 4, tt * 128:(tt + 1) * 128],
                   psum[b][:, :].rearrange("p (j t) -> p j t", j=4), [psb[b]], hTb[half * 4:(half + 1) * 4],
                   eng=("act" if half == 0 else "dve"))
        for L in layers:
            if L % 2 == 0:
                layer_even(L, st)
            else:
                layer_odd(L, st)
        for tt in range(4):
            tok = st * ST + tt * 128
            for half in range(2):
                b = (tt * 2 + half) % 8
                hi = tt * 2 + half
                for j in range(4):
                    k = half * 4 + j
                    TR(psum[b][:, j * 128:(j + 1) * 128], hT[:, k, tt * 128:(tt + 1) * 128], ident,
                       [hTb[k], cstfb], [psb[b]], j == 3)
                if hi < 4:
                    sap, sbf, key = fs[hi][:, :], [fsb[hi]], f"fs{hi}"
                else:
                    c0 = 2 * (hi - 4)
                    sap = catT[:, c0:c0 + 2, :].rearrange("p a b -> p (a b)").bitcast(F32)
                    sbf, key = catb[c0:c0 + 2], f"cst{hi}"
                CP(sap, psum[b][:, :], [psb[b]], sbf, eng=("act" if half == 0 else "dve"))
                ev = DMA("sp", key, out_d[tok:tok + 128, half * 512:(half + 1) * 512], sap, sbf, ())
                sc.final.append(ev)

    last = {}
    for key, sem, val in sc.final:
        last[key] = (sem, val)
    sc.q["sp"].append((list(last.values()), lambda e: e.nop(), None))

    with nc.Block() as block:
        @block.sync
        def _(e):
            sc.emit("sp", e)

        @block.tensor
        def _(e):
            sc.emit("pe", e)

        @block.vector
        def _(e):
            sc.emit("dve", e)

        @block.scalar
        def _(e):
            sc.emit("act", e)

        @block.gpsimd
        def _(e):
            sc.emit("pool", e)
    return nc, sc


def _pack_inputs(inp):
    f = lambda a: np.ascontiguousarray(np.asarray(a, dtype=np.float32))
    P = {}
    cstf = np.zeros((128, CF_N), np.float32)
    cstf[:, CF_IDENT:CF_IDENT + 128] = np.eye(128)
    cstf[:, CF_ONES:CF_ONES + 128] = 1.0
    cstf[:, CF_IDH:CF_IDH + 128] = 0.5 * np.eye(128)
    cstb = np.zeros((128, CB_N), np.float32)
    s_ = np.arange(128)[:, None]
    t_ = np.arange(128)[None, :]
    cstb[:, CB_MCUR:CB_MCUR + 128] = (t_ >= s_)
    cstb[:, CB_MPREV:CB_MPREV + 128] = (s_ > t_)
    cstb[:, CB_B64:CB_B64 + 128] = ((s_ // 64) == (t_ // 64)) / 64.0
    cstb[:, CB_O1024:CB_O1024 + 128] = 1.0 / 1024.0
    cstb[:, CB_O512:CB_O512 + 128] = 1.0 / 512.0
    cstb[:, CB_OPAD:CB_OPAD + 64] = 1.0
    cstb[:, CB_OPAD + 128 + 64:CB_OPAD + 256] = 1.0
    cstb[:, CB_IDB:CB_IDB + 128] = np.eye(128)
    cstb[:, CB_NCUR:CB_NCUR + 128] = np.where(t_ >= s_, 0.0, -2400.0)
    cstb[:, CB_NPREV:CB_NPREV + 128] = np.where(s_ > t_, 0.0, -2400.0)
    pfirst = np.zeros((128, 512), np.float32)
    for g in range(4):
        w = 2 << g
        inwin = ((t_ - s_) >= 0) & ((t_ - s_) < w)
        cstb[:, CB_PCUR + g * 128:CB_PCUR + (g + 1) * 128] = np.where(inwin, 1.0 / w, 0.0) - (s_ == t_)
        cstb[:, CB_PPREV + g * 128:CB_PPREV + (g + 1) * 128] = np.where((t_ + 128 - s_) < w, 1.0 / w, 0.0)
        pfirst[:, g * 128:(g + 1) * 128] = np.where(inwin, 1.0 / np.minimum(t_ + 1, w), 0.0) - (s_ == t_)
    P["cstf"], P["cstb"], P["pfirst"] = cstf, cstb, pfirst
    prow = np.zeros((128, D), np.float32)
    prow[R_NORM:R_NORM + 4] = f(inp["norm_g"])
    prow[R_MEMG] = f(inp["mem_norm_g"])
    for i in range(2):
        prow[R_ALNG + i, :512] = f(inp["a_ln_g"])[i]
        prow[R_ALNB + i, :512] = f(inp["a_ln_b"])[i]
        prow[R_BSC + i, :512] = f(inp["b_scale"])[i]
        prow[R_DWB + i, :512] = f(inp["d_dw_b"])[i]
        prow[R_DLNG + i, :512] = f(inp["d_ln_g"])[i]
        prow[R_DLNB + i, :512] = f(inp["d_ln_b"])[i]
        prow[R_CQ + i, :128] = np.tile(f(inp["c_qnorm"])[i], 2)
        prow[R_CK + i, :128] = np.tile(f(inp["c_knorm"])[i], 2)
        prow[R_SINK + i, :512] = np.repeat(f(inp["c_sink"])[i], 64)
        prow[R_DW + 32 * i:R_DW + 32 * i + 31, :512] = f(inp["d_dw"])[i]
    for L in range(4):
        prow[R_MQ + L, :128] = np.tile(f(inp["m_qnorm"])[L], 2)
        prow[R_MK + L, :128] = np.tile(f(inp["m_knorm"])[L], 2)
    P["prow"] = prow

    def pk(w, colidx):
        sub = w[:, colidx].reshape(KC, 128, len(colidx)).transpose(1, 0, 2)
        return np.ascontiguousarray(sub).reshape(128, -1)

    winp = np.zeros((26, 128, KC * 512), np.float32)
    for L in range(4):
        i = L // 2
        if L % 2 == 0:
            w = f(inp["w_in_even"])[i]
            for g in range(6):
                winp[WIN_BASE[L] + g] = pk(w, np.arange(g * 512, (g + 1) * 512))
        else:
            w = f(inp["w_in_odd"])[i]
            segs = {"q": (0, 512), "gc": (768, 1280), "da": (1280, 1792), "db": (1792, 2304), "gd": (2304, 2816),
                    "m": (2816, 3328)}
            for nm, (a, b) in segs.items():
                winp[WIN_BASE[L] + ODD_G[nm]] = pk(w, np.arange(a, b))
            idx = np.concatenate([np.arange(512, 576), np.arange(512, 576), np.arange(576, 640), np.arange(576, 640),
                                  np.arange(640, 768)])
            tmp = np.zeros((128, KC, 512), np.float32)
            tmp[:, :, :384] = pk(w, idx).reshape(128, KC, 384)
            winp[WIN_BASE[L] + ODD_G["kv"]] = tmp.reshape(128, -1)
    P["winp"] = winp
    woutp = np.zeros((16, 128, 10 * 256), np.float32)
    for L in range(4):
        w = f(inp["w_out_even"] if L % 2 == 0 else inp["w_out_odd"])[L // 2]
        for dp in range(4):
            sub = w[:, dp * 256:(dp + 1) * 256].reshape(10, 128, 256).transpose(1, 0, 2)
            woutp[L * 4 + dp] = np.ascontiguousarray(sub).reshape(128, -1)
    P["woutp"] = woutp
    P["wkvp"] = np.stack([pk(f(inp["w_mem_kv"])[L], np.arange(512)) for L in range(4)])
    P["pwp"] = np.stack([np.ascontiguousarray(f(inp["d_pw"])[i].reshape(4, 128, 512).transpose(1, 0, 2)).reshape(128, -1)
                         for i in range(2)])
    P["bwp"] = np.stack([np.ascontiguousarray(f(inp["b_w"])[i].transpose(1, 0, 2)).reshape(128, -1) for i in range(2)])
    P["wstp"] = np.stack([np.ascontiguousarray(f(inp["a_ws"])[i].transpose(2, 0, 1)).reshape(128, -1) for i in range(2)])
    P["absb"] = np.stack([np.ascontiguousarray(np.broadcast_to(f(inp["a_bs"])[i].reshape(1, -1), (128, 1024)))
                          for i in range(2)])
    return P


_CACHE = {}


def run_layers(inputs, layers, trace=False):
    key = tuple(layers)
    if key not in _CACHE:
        _CACHE[key] = build_program(list(layers))
    nc, sc = _CACHE[key]
    x = np.ascontiguousarray(np.asarray(inputs["x"], dtype=np.float32))
    mem = np.ascontiguousarray(np.asarray(inputs["mem"], dtype=np.float32))
    P = _pack_inputs(inputs)
    in_maps = []
    for b in range(NCORES):
        m = {"x": x[b], "mem": mem[b]}
        m.update(P)
        in_maps.append(m)
    res = run_bass_kernel_spmd(nc, in_maps, core_ids=list(range(NCORES)), trace=trace)
    outp = np.stack([np.asarray(r["out"]) for r in res.results], axis=0)
    return outp.astype(np.float32), res


def kernel(**inputs):
    outp, _ = run_layers(inputs, (0, 1, 2, 3))
    return outp
```

```python
import numpy as np
import concourse.bass as bass
import concourse.mybir as mybir
from concourse.bass_utils import run_bass_kernel_spmd

F32 = mybir.dt.float32
BF16 = mybir.dt.bfloat16
ALU = mybir.AluOpType
AF = mybir.ActivationFunctionType

D = 1024
S = 4096
KC = 8
ST = 512
NST = S // ST
NCORES = 8
EPS = 1e-6
NMEM = 256

R_NORM, R_MEMG, R_ALNG, R_ALNB, R_BSC, R_DWB, R_DLNG, R_DLNB = 0, 4, 5, 7, 9, 11, 13, 15
R_MQ, R_MK, R_CQ, R_CK, R_SINK, R_DW = 17, 21, 25, 27, 29, 32
CF_IDENT, CF_ONES, CF_RC16, CF_IDH, CF_N = 0, 128, 256, 320, 448
CB_MCUR, CB_MPREV, CB_B64, CB_O1024, CB_O512, CB_OPAD = 0, 128, 256, 384, 512, 640
CB_IDB, CB_NCUR, CB_NPREV, CB_PCUR, CB_PPREV, CB_N = 896, 1024, 1152, 1280, 1792, 2304
EVEN_G = {"u": 0, "v": 1, "ga": 2, "xb": 3, "gb": 4, "m": 5}
ODD_G = {"q": 0, "gc": 1, "da": 2, "db": 3, "gd": 4, "m": 5, "kv": 6}
WIN_BASE = [0, 6, 13, 19]


class Buf:
    __slots__ = ("name", "w", "r")

    def __init__(self, name):
        self.name = name
        self.w = None
        self.r = []


class Sched:
    ENG = ("pe", "dve", "act", "pool", "sp")

    def __init__(self, nc):
        self.nc = nc
        self.q = {e: [] for e in self.ENG}
        self.sem = {e: nc.alloc_semaphore("c_" + e) for e in ("pe", "dve", "act", "pool")}
        self.cnt = {e: 0 for e in self.sem}
        self.known = {e: {} for e in self.ENG}
        self.dsem = {}
        self.dtot = {}
        self.final = []
        self.nins = 0

    def _deps(self, eng, R, W):
        need = {}

        def add(ev, raw):
            key, sem, val = ev
            if key == eng and eng == "pe":
                return
            if self.known[eng].get(key, 0) >= val:
                return
            if key not in need or need[key][1] < val:
                need[key] = (sem, val)

        for b in R:
            if b.w is not None:
                add(b.w, True)
        for b in W:
            if b.w is not None:
                add(b.w, False)
            for ev in b.r:
                add(ev, False)
        for key, (sem, val) in need.items():
            self.known[eng][key] = val
        return list(need.values())

    def _record(self, ev, R, W):
        for b in R:
            b.r.append(ev)
        for b in W:
            b.w = ev
            b.r = []

    def op(self, eng, fn, R=(), W=(), inc=True):
        waits = self._deps(eng, R, W)
        sem = self.sem[eng]
        if inc:
            self.cnt[eng] += 1
            val = self.cnt[eng]
        else:
            assert eng == "pe"
            val = self.cnt[eng] + 1
        self._record((eng, sem, val), R, W)
        self.q[eng].append((waits, fn, (sem, 1) if inc else None))
        self.nins += 1

    def dma(self, queue, key, fn, R=(), W=()):
        waits = self._deps(queue, R, W)
        if key not in self.dsem:
            self.dsem[key] = self.nc.alloc_semaphore("d_" + key)
            self.dtot[key] = 0
        self.dtot[key] += 16
        ev = (("d", key), self.dsem[key], self.dtot[key])
        self._record(ev, R, W)
        self.q[queue].append((waits, fn, (self.dsem[key], 16)))
        self.nins += 1
        return ev

    def emit(self, eng, e):
        for waits, fn, inc in self.q[eng]:
            for sem, val in waits:
                e.wait_ge(sem, val)
            ins = fn(e)
            if inc is not None:
                ins.then_inc(inc[0], inc[1])


def build_program(layers):
    nc = bass.Bass("TRN2", target_bir_lowering=False)

    def din(name, shape):
        return nc.dram_tensor(name, list(shape), F32, kind="ExternalInput").ap()

    x_d = din("x", (S, D))
    mem_d = din("mem", (NMEM, D))
    prow_d = din("prow", (128, D))
    cstf_d = din("cstf", (128, CF_N))
    cstb_d = din("cstb", (128, CB_N))
    winp_d = din("winp", (26, 128, KC * 512))
    woutp_d = din("woutp", (16, 128, 10 * 256))
    wkvp_d = din("wkvp", (4, 128, KC * 512))
    pwp_d = din("pwp", (2, 128, 4 * 512))
    bwp_d = din("bwp", (2, 128, 4 * 128))
    wstp_d = din("wstp", (2, 128, 8 * 128))
    absb_d = din("absb", (2, 128, 8 * 128))
    pfirst_d = din("pfirst", (128, 512))
    out_d = nc.dram_tensor("out", [S, D], F32, kind="ExternalOutput").ap()

    sc = Sched(nc)

    def sb(name, shape, dt):
        return nc.alloc_sbuf_tensor("sb_" + name, list(shape), dt)

    def bufs(name, n):
        return [Buf(f"{name}{i}") for i in range(n)]

    hT = sb("hT", (128, KC, ST), F32)
    hTb = bufs("hT", KC)
    xnT = sb("xnT", (128, KC, ST), BF16)
    xnb = bufs("xn", KC)
    stage = [xnT[:, 4 * i:4 * i + 4, :].rearrange("p a b -> p (a b)").bitcast(F32) for i in range(2)]
    stageb = [xnb[0:4], xnb[4:8]]
    cstf = sb("cstf", (128, CF_N), F32)
    cstfb = Buf("cstf")
    cstb = sb("cstb", (128, CB_N), BF16)
    cstbb = Buf("cstb")
    ident = cstf[:, CF_IDENT:CF_IDENT + 128]
    onesf = cstf[:, CF_ONES:CF_ONES + 128]
    identh = cstf[:, CF_IDH:CF_IDH + 128]
    mcur = cstb[:, CB_MCUR:CB_MCUR + 128]
    mprev = cstb[:, CB_MPREV:CB_MPREV + 128]
    b64 = cstb[:, CB_B64:CB_B64 + 128]
    o1024 = cstb[:, CB_O1024:CB_O1024 + 128]
    o512 = cstb[:, CB_O512:CB_O512 + 128]
    opad = cstb[:, CB_OPAD:CB_OPAD + 256].rearrange("p (j c) -> p j c", j=2)
    identb = cstb[:, CB_IDB:CB_IDB + 128]
    mneg_cur = cstb[:, CB_NCUR:CB_NCUR + 128]
    mneg_prev = cstb[:, CB_NPREV:CB_NPREV + 128]
    pcur = cstb[:, CB_PCUR:CB_PCUR + 512].rearrange("p (g t) -> p g t", g=4)
    pprev = cstb[:, CB_PPREV:CB_PPREV + 512].rearrange("p (g t) -> p g t", g=4)
    cols = sb("cols", (128, KC, 128), F32)
    colsb = Buf("cols")
    esk = sb("esk", (128, 2, 4), F32)
    eskb = Buf("esk")
    KTpad = sb("KTpad", (128, 4, 4, NMEM), BF16)
    Vpad = sb("Vpad", (128, 4, 2, 4, 128), BF16)
    KTb = bufs("KT", 4)
    Vpb = bufs("Vp", 4)
    win = [sb(f"win{i}", (128, KC, 512), BF16) for i in range(3)]
    winb = [bufs(f"win{i}h", 2) for i in range(3)]
    wout = [sb(f"wout{i}", (128, 10, 256), BF16) for i in range(3)]
    woutb = bufs("wout", 3)
    diag = sb("diag", (128, 124, 128), BF16)
    diagb = Buf("diag")
    pw = sb("pw", (128, 4, 512), BF16)
    pwb = Buf("pw")
    wsT2 = [sb(f"wsT{i}", (128, 8, 128), BF16) for i in range(2)]
    wsTb2 = bufs("wsT", 2)
    BT2 = [sb(f"BT{i}", (128, 4, 128), F32) for i in range(2)]
    BTb2 = bufs("BT", 2)
    bw = pw[:, 0, :].rearrange("p (g d) -> p g d", g=4)
    bwb = pwb
    carryK = [sb(f"carryK{i}", (128, 4, 128), BF16) for i in range(2)]
    carryV = [sb(f"carryV{i}", (128, 4, 128), BF16) for i in range(2)]
    carryH = [sb(f"carryH{i}", (128, 4, 32), BF16) for i in range(2)]
    carryX = [sb(f"carryX{i}", (128, 512), BF16) for i in range(2)]
    cKb, cVb, cHb, cXb = bufs("cK", 2), bufs("cV", 2), bufs("cH", 2), bufs("cX", 2)
    sq = [sb(f"sq{i}", (128, ST), BF16) for i in range(4)]
    sqb = bufs("sq", 4)
    fs = [sb(f"fs{i}", (128, ST), F32) for i in range(4)]
    fsb = bufs("fs", 4)
    catT = sb("catT", (128, 10, ST), BF16)
    catb = bufs("cat", 10)
    sgA = sb("sgA", (128, 4, ST), BF16)
    sgAb = bufs("sgA", 4)
    sgB = sb("sgB", (128, 4, ST), BF16)
    sgBb = bufs("sgB", 4)
    sgM = sb("sgM", (128, 2, ST), BF16)
    sgMb = bufs("sgM", 2)
    qmn = sb("qmn", (128, 2, ST), BF16)
    qmnb = bufs("qmn", 2)
    PT = [sb(f"PT{i}", (128, 1024), BF16) for i in range(4)]
    PTb = bufs("PT", 4)
    diffT = [sb(f"diffT{i}", (128, ST), BF16) for i in range(4)]
    dfb = bufs("diff", 4)
    stt = [sb(f"stt{i}", (128, 16), F32) for i in range(2)]
    sttb = bufs("stt", 2)
    qnT = sb("qnT", (128, 4, ST), BF16)
    qnb = bufs("qn", 4)
    Kz = sb("Kz", (128, 4, 5 * 128), BF16)
    Kzb = bufs("Kz", 5)
    Vz = sb("Vz", (128, 5, 4, 128), BF16)
    Vzb = bufs("Vz", 5)
    xbT = Vz[:, :, :, :].rearrange("p s a c -> p s (a c)")
    xbTb = Vzb
    h1T = sb("h1T", (128, 4, 32 + ST), BF16)
    h1b = bufs("h1", 4)
    hn = sb("hn", (128, 4, ST), BF16)
    hnb = bufs("hn", 4)
    vnpad = [qnT[:, 0:2, :].rearrange("p a (h c) -> p (a h) c", c=128),
             qnT[:, 2:4, :].rearrange("p a (h c) -> p (a h) c", c=128),
             hn[:, 0:2, :].rearrange("p a (h c) -> p (a h) c", c=128),
             hn[:, 2:4, :].rearrange("p a (h c) -> p (a h) c", c=128)]
    vnb = [qnb[0:2], qnb[2:4], hnb[0:2], hnb[2:4]]
    stage.append(h1T[:, :, :].rearrange("p a b -> p (a b)")[:, 0:2048].bitcast(F32))
    stageb.append(h1b)
    stage.append(hn[:, :, :].rearrange("p a b -> p (a b)").bitcast(F32))
    stageb.append(hnb)
    x3_prefetched = [False]
    psum = [nc.alloc_psum_tensor(f"ps{i}", [128, 512], F32) for i in range(8)]
    psb = bufs("ps", 8)

    def MM(out, lhsT, rhs, start, stop, R, W, inc):
        sc.op("pe", lambda e: e.matmul(out, lhsT, rhs, start=start, stop=stop), R, W, inc)

    def TR(out, in_, idn, R, W, inc):
        sc.op("pe", lambda e: e.transpose(out=out, in_=in_, identity=idn), R, W, inc)

    def ACT(out, in_, func, R, W, bias=None, scale=None, accum_out=None):
        kw = {}
        if bias is not None:
            kw["bias"] = bias
        if scale is not None:
            kw["scale"] = scale
        if accum_out is not None:
            kw["accum_out"] = accum_out
        sc.op("act", lambda e: e.activation(out=out, in_=in_, func=func, **kw), R, W)

    def TT(out, in0, in1, op, R, W, eng="dve"):
        sc.op(eng, lambda e: e.tensor_tensor(out=out, in0=in0, in1=in1, op=op), R, W)

    def STT(out, in0, scalar, in1, op0, op1, R, W, eng="dve"):
        sc.op(eng, lambda e: e.scalar_tensor_tensor(out=out, in0=in0, scalar=scalar, in1=in1, op0=op0, op1=op1), R, W)

    def TS(out, in0, s1, s2, op0, op1, R, W, eng="dve"):
        if op1 is None:
            sc.op(eng, lambda e: e.tensor_scalar(out=out, in0=in0, scalar1=s1, scalar2=None, op0=op0), R, W)
        else:
            sc.op(eng, lambda e: e.tensor_scalar(out=out, in0=in0, scalar1=s1, scalar2=s2, op0=op0, op1=op1), R, W)

    def CP(out, in_, R, W, eng="dve"):
        if eng == "act":
            sc.op("act", lambda e: e.copy(out=out, in_=in_), R, W)
        else:
            sc.op(eng, lambda e: e.tensor_copy(out=out, in_=in_), R, W)

    def RCP(out, in_, R, W):
        sc.op("dve", lambda e: e.reciprocal(out=out, in_=in_), R, W)

    def MEMSET(ap, val, W, eng="dve"):
        sc.op(eng, lambda e: e.memset(ap, val), (), W)

    def DMA(queue, key, out, in_, R, W):
        return sc.dma(queue, key, lambda e: e.dma_start(out=out, in_=in_), R, W)

    def col(c, r):
        return cols[:, c, r:r + 1]

    def rsqrt_eps(out, in_, R, W):
        ACT(out, in_, AF.Ln, R, W, bias=EPS)
        ACT(out, out, AF.Exp, W, W, scale=-0.5)

    wseq = []
    for st in range(NST):
        for L in layers:
            gmap = EVEN_G if L % 2 == 0 else ODD_G
            order = (["m", "ga", "gb", "v", "u", "xb"] if L % 2 == 0
                     else ["db", "da", "gc", "gd", "m", "kv", "q"])
            for nm in order:
                wseq.append(("win", WIN_BASE[L] + gmap[nm]))
            for dp in range(4):
                wseq.append(("wout", L * 4 + dp))
    wptr = {"win": 0, "wout": 0}
    wlist = {"win": [w for w in wseq if w[0] == "win"], "wout": [w for w in wseq if w[0] == "wout"]}
    wemitted = {"win": 0, "wout": 0}
    bg = []

    def bg_flush(n):
        while bg and n > 0:
            fn, R, W = bg.pop(0)
            sc.op("pool", fn, R, W)
            n -= 1

    def w_emit(kind, upto):
        ring, rb, src = (win, winb, winp_d) if kind == "win" else (wout, woutb, woutp_d)
        n = len(ring)
        while wemitted[kind] < min(upto, len(wlist[kind])):
            i = wemitted[kind]
            gi = wlist[kind][i][1]
            slot = i % n
            if kind == "win":
                sv = src[gi].rearrange("p (k c) -> p k c", k=KC)
                for h_ in range(2):
                    DMA("pool", f"win{slot}h{h_}", ring[slot][:, :, h_ * 256:(h_ + 1) * 256],
                        sv[:, :, h_ * 256:(h_ + 1) * 256], (), [rb[slot][h_]])
            else:
                dst = ring[slot][:, :, :].rearrange("p a b -> p (a b)")
                DMA("pool", f"{kind}{slot}", dst, src[gi], (), [rb[slot]])
            wemitted[kind] += 1
            bg_flush(1)

    def prefetch_wout():
        w_emit("wout", wptr["wout"] + 3)

    def w_get(kind, gi, hold=0):
        i = wptr[kind]
        assert wlist[kind][i][1] == gi, (kind, i, wlist[kind][i], gi)
        n = len(win) if kind == "win" else len(wout)
        w_emit(kind, i + n - hold)
        wptr[kind] += 1
        slot = i % n
        return (win[slot], winb[slot]) if kind == "win" else (wout[slot], woutb[slot])

    def inproj_fm(wt, wbuf, chunks, banks, kouter=False):
        if kouter:
            for k in range(KC):
                for c, b in zip(chunks, banks):
                    MM(psum[b][:, :], wt[:, k, c * 128:(c + 1) * 128], xnT[:, k, :], k == 0, k == KC - 1,
                       [wbuf[c // 2], xnb[k]], [psb[b]], k == KC - 1)
            return
        for c, b in zip(chunks, banks):
            for k in range(KC):
                MM(psum[b][:, :], wt[:, k, c * 128:(c + 1) * 128], xnT[:, k, :], k == 0, k == KC - 1,
                   [wbuf[c // 2], xnb[k]], [psb[b]], k == KC - 1)

    rms_ready = [False]
    prefetch_hook = [None]

    def rms_norm_x(L):
        if not rms_ready[0]:
            for k in range(KC):
                ACT(sq[k % 2][:, :], hT[:, k, :], AF.Square, [hTb[k]], [sqb[k % 2]])
                MM(psum[7][:, :], o1024, sq[k % 2][:, :], k == 0, k == KC - 1, [cstbb, sqb[k % 2]], [psb[7]], True)
        rms_ready[0] = False
        rsqrt_eps(fs[0][:, :], psum[7][:, :], [psb[7]], [fsb[0]])
        for k in range(KC):
            STT(xnT[:, k, :], hT[:, k, :], col(k, R_NORM + L), fs[0][:, :], ALU.mult, ALU.mult,
                [hTb[k], colsb, fsb[0]], [xnb[k]])

    def out_proj(L, with_rms, border=(0, 1, 2, 3, 4, 5, 6, 0), korder=tuple(range(10))):
        pend = []

        def ss_mm():
            d_ = pend.pop(0)
            MM(psum[7][:, :], o1024, sq[d_ % 2][:, :], d_ == 0, d_ == KC - 1, [cstbb, sqb[d_ % 2]], [psb[7]], True)

        def finish(d, b):
            if pend:
                ss_mm()
            TT(hT[:, d, :], hT[:, d, :], psum[b][:, :], ALU.add, [psb[b], hTb[d]], [hTb[d]])
            if with_rms:
                ACT(sq[d % 2][:, :], hT[:, d, :], AF.Square, [hTb[d]], [sqb[d % 2]])
                pend.append(d)

        wts = [w_get("wout", L * 4 + dp, hold=dp) for dp in range(2)]
        for ki, k in enumerate(korder):
            for d in range(4):
                wt, wbuf = wts[d // 2]
                b = border[d]
                MM(psum[b][:, :], wt[:, k, (d % 2) * 128:(d % 2 + 1) * 128], catT[:, k, :], ki == 0, ki == 9,
                   [wbuf, catb[k]], [psb[b]], ki == 9)
        for d in range(4):
            finish(d, border[d])
        for dp in range(2, 4):
            wt, wbuf = w_get("wout", L * 4 + dp)
            for half in range(2):
                d = dp * 2 + half
                b = border[d]
                for k in range(10):
                    MM(psum[b][:, :], wt[:, k, half * 128:(half + 1) * 128], catT[:, k, :], k == 0, k == 9,
                       [wbuf, catb[k]], [psb[b]], k == 9)
                finish(d, b)
        while pend:
            ss_mm()
        rms_ready[0] = with_rms

    def head_norm(src_bank, ssbank, fsi, qcol_r, outs):
        si = ssbank % 2
        ACT(sq[si][:, :], psum[src_bank][:, :], AF.Square, [psb[src_bank]], [sqb[si]])
        MM(psum[ssbank][:, :], b64, sq[si][:, :], True, True, [cstbb, sqb[si]], [psb[ssbank]], True)
        rsqrt_eps(fs[fsi][:, :], psum[ssbank][:, :], [psb[ssbank]], [fsb[fsi]])
        for out_ap, psl, W in outs:
            STT(out_ap, psum[src_bank][psl, :], cols[psl, 0, qcol_r:qcol_r + 1], fs[fsi][psl, :],
                ALU.mult, ALU.mult, [psb[src_bank], colsb, fsb[fsi]], W)

    def head_norm_multi(items):
        for src, ssb, sqi, fsi, cr, outs in items:
            ACT(sq[sqi][:, :], psum[src][:, :], AF.Square, [psb[src]], [sqb[sqi]])
        for src, ssb, sqi, fsi, cr, outs in items:
            MM(psum[ssb][:, :], b64, sq[sqi][:, :], True, True, [cstbb, sqb[sqi]], [psb[ssb]], True)
        for src, ssb, sqi, fsi, cr, outs in items:
            ACT(fs[fsi][:, :], psum[ssb][:, :], AF.Ln, [psb[ssb]], [fsb[fsi]], bias=EPS)
        for src, ssb, sqi, fsi, cr, outs in items:
            ACT(fs[fsi][:, :], fs[fsi][:, :], AF.Exp, [fsb[fsi]], [fsb[fsi]], scale=-0.5)
        for src, ssb, sqi, fsi, cr, outs in items:
            for out_ap, psl, W in outs:
                STT(out_ap, psum[src][psl, :], cols[psl, 0, cr:cr + 1], fs[fsi][psl, :],
                    ALU.mult, ALU.mult, [psb[src], colsb, fsb[fsi]], W)

    def recip_act(out, in_, R, W):
        ACT(out, in_, AF.Ln, R, W)
        ACT(out, out, AF.Exp, W, W, scale=-1.0)

    def mem_head(L, qbanks, gbanks, phase=None, ssbanks=None):
        if ssbanks is None:
            ssbanks = gbanks
        if phase is None:
            for p in range(2):
                ACT(sgM[:, p, :], psum[gbanks[p]][:, :], AF.Silu, [psb[gbanks[p]]], [sgMb[p]])
        if phase in (None, 0):
            for p in range(2):
                ACT(sq[p][:, :], psum[qbanks[p]][:, :], AF.Square, [psb[qbanks[p]]], [sqb[p]])
        if phase == 2:
            for p in range(2):
                ACT(sgM[:, p, :], psum[gbanks[p]][:, :], AF.Silu, [psb[gbanks[p]]], [sgMb[p]])
        gbanks = ssbanks
        if phase in (None, 1):
            for p in range(2):
                MM(psum[gbanks[p]][:, :], b64, sq[p][:, :], True, True, [cstbb, sqb[p]], [psb[gbanks[p]]], True)
            for p in range(2):
                ACT(fs[2 + p][:, :], psum[gbanks[p]][:, :], AF.Ln, [psb[gbanks[p]]], [fsb[2 + p]], bias=EPS)
            for p in range(2):
                ACT(fs[2 + p][:, :], fs[2 + p][:, :], AF.Exp, [fsb[2 + p]], [fsb[2 + p]], scale=-0.5)
            for p in range(2):
                STT(qmn[:, p, :], psum[qbanks[p]][:, :], col(0, R_MQ + L), fs[2 + p][:, :], ALU.mult, ALU.mult,
                    [psb[qbanks[p]], colsb, fsb[2 + p]], [qmnb[p]])

    def mem_attn(L, qbanks, gbanks, sbanks, part=None):
        if part in (None, 1):
            mem_scores(L, sbanks)
        if part in (None, 2):
            mem_pv(L, qbanks, gbanks)

    def mem_scores(L, sbanks):
        for h in range(4):
            ptv = PT[h][:, :].rearrange("p (a b) -> p a b", a=2)
            for mh in range(2):
                b = sbanks[(h * 2 + mh) % len(sbanks)]
                MM(psum[b][:, :], KTpad[:, L, h, mh * 128:(mh + 1) * 128], qmn[:, h // 2, :], True, True,
                   [KTb[L], qmnb[h // 2]], [psb[b]], True)
                ACT(ptv[:, mh, :], psum[b][:, :], AF.Exp, [psb[b]], [PTb[h]], scale=0.125)

    def mem_pv(L, qbanks, gbanks):
        for p in range(2):
            ob, db_ = gbanks[p], qbanks[p]
            n = 0
            for j in range(2):
                h = 2 * p + j
                ptv = PT[h][:, :].rearrange("p (a b) -> p a b", a=2)
                for mh in range(2):
                    MM(psum[ob][:, :], Vpad[:, L, mh, h, :], ptv[:, mh, :], n == 0, n == 3,
                       [Vpb[L], PTb[h]], [psb[ob]], n == 3)
                    n += 1
            n = 0
            for j in range(2):
                h = 2 * p + j
                ptv = PT[h][:, :].rearrange("p (a b) -> p a b", a=2)
                for mh in range(2):
                    MM(psum[db_][:, :], opad[:, j, :], ptv[:, mh, :], n == 0, n == 3,
                       [cstbb, PTb[h]], [psb[db_]], n == 3)
                    n += 1
            recip_act(fs[2][:, :], psum[db_][:, :], [psb[db_]], [fsb[2]])
            TT(fs[3][:, :], psum[ob][:, :], fs[2][:, :], ALU.mult, [psb[ob], fsb[2]], [fsb[3]])
            TT(catT[:, 8 + p, :], fs[3][:, :], sgM[:, p, :], ALU.mult, [fsb[3], sgMb[p]], [catb[8 + p]])

    def layer_even(L, st):
        i = L // 2
        first = (st == 0)
        base = WIN_BASE[L]
        wsT, wsTb, BT, BTb = wsT2[i], wsTb2[i], BT2[i], BTb2[i]
        DMA("pool", "pw", pw[:, 0, :], bwp_d[i], (), [bwb])
        rms_norm_x(L)
        for tb in range(4):
            MEMSET(vnpad[tb], 0.0, vnb[tb])
        if not first:
            CP(xbT[:, 0, :], carryX[i][:, :], [cXb[i]], [xbTb[0]])
        else:
            DMA("pool", "pfirst", PT[3][:, 0:512], pfirst_d, (), [PTb[3]])
        wt, wbuf = w_get("win", base + EVEN_G["m"])
        inproj_fm(wt, wbuf, range(4), [0, 1, 2, 3], kouter=True)
        mem_head(L, [0, 1], [2, 3], phase=0)
        wtg, wbufg = w_get("win", base + EVEN_G["ga"])
        inproj_fm(wtg, wbufg, [0], [4])
        mem_head(L, [0, 1], [2, 3], phase=1, ssbanks=[6, 7])
        inproj_fm(wtg, wbufg, [1, 2, 3], [5, 6, 7])
        mem_head(L, [0, 1], [2, 3], phase=2)
        for c in range(4):
            ACT(sgA[:, c, :], psum[4 + c][:, :], AF.Silu, [psb[4 + c]], [sgAb[c]])
        wt2, wbuf2 = w_get("win", base + EVEN_G["gb"])
        inproj_fm(wt2, wbuf2, range(4), [4, 5, 6, 7])
        for c in range(4):
            ACT(sgB[:, c, :], psum[4 + c][:, :], AF.Silu, [psb[4 + c]], [sgBb[c]])
        wt, wbuf = w_get("win", base + EVEN_G["v"])
        for tb in range(4):
            b = tb
            for k in range(KC):
                MM(psum[b][:, :], xnT[:, k, tb * 128:(tb + 1) * 128], wt[:, k, :], k == 0, k == KC - 1,
                   wbuf + [xnb[k]], [psb[b]], k == KC - 1)
            s_ = stt[tb % 2]
            sb_ = sttb[tb % 2]
            sc.op("dve", lambda e, s_=s_, b=b: e.bn_stats(out=s_[:, 0:6], in_=psum[b][:, :]), [psb[b]], [sb_])
            sc.op("dve", lambda e, s_=s_: e.bn_aggr(out=s_[:, 6:8], in_=s_[:, 0:6]), [sb_], [sb_])
            rsqrt_eps(s_[:, 8:9], s_[:, 7:8], [sb_], [sb_])
            STT(s_[:, 9:10], s_[:, 6:7], -1.0, s_[:, 8:9], ALU.mult, ALU.mult, [sb_], [sb_])
            vp = vnpad[tb].rearrange("p (q j) c -> p q j c", j=2)
            pv = psum[b][:, :].rearrange("p (q j d) -> p q j d", q=4, j=2)
            for j in range(2):
                ACT(vp[:, :, j, 64 * j:64 * j + 64], pv[:, :, j, :], AF.Identity, [psb[b], sb_], vnb[tb],
                    bias=s_[:, 9:10], scale=s_[:, 8:9])
        prefetch_wout()
        wtu, wbufu = w_get("win", base + EVEN_G["u"])
        inproj_fm(wtu, wbufu, range(4), [4, 5, 6, 7])
        for tb in range(4):
            for p in range(4):
                for j in range(2):
                    h = 2 * p + j
                    MM(psum[p][:, tb * 128:(tb + 1) * 128], vnpad[tb][:, h, :], wsT[:, h, :], j == 0, j == 1,
                       vnb[tb] + [wsTb], [psb[p]], j == 1)
        for p in range(4):
            t1 = fs[2 + (p % 2)]
            t1b = fsb[2 + (p % 2)]
            STT(t1[:, :].rearrange("p (a b) -> p a b", a=4), psum[p][:, :].rearrange("p (a b) -> p a b", a=4),
                col(p, R_ALNG + i), BT[:, p, :].unsqueeze(1).broadcast_to([128, 4, 128]), ALU.mult, ALU.add,
                [psb[p], colsb, BTb], [t1b])
            TT(t1[:, :], t1[:, :], sgA[:, p, :], ALU.mult, [t1b, sgAb[p]], [t1b])
            TT(catT[:, p, :], t1[:, :], psum[4 + p][:, :], ALU.mult, [t1b, psb[4 + p]], [catb[p]])
        wt, wbuf = w_get("win", base + EVEN_G["xb"])
        for tb in range(4):
            b = tb
            for k in range(KC):
                MM(psum[b][:, :], xnT[:, k, tb * 128:(tb + 1) * 128], wt[:, k, :], k == 0, k == KC - 1,
                   wbuf + [xnb[k]], [psb[b]], k == KC - 1)
            CP(xbT[:, 1 + tb, :], psum[b][:, :], [psb[b]], [xbTb[1 + tb]], eng="dve")
        CP(carryX[i][:, :], xbT[:, 4, :], [xbTb[4]], [cXb[i]])
        if not first:
            mem_attn(L, [4, 5], [6, 7], [0, 1, 2, 3, 4, 5, 6, 7], part=1)
        for g in range(4):
            for tb in range(4):
                o_ = psum[g][:, tb * 128:(tb + 1) * 128]
                if first and tb == 0:
                    MM(o_, xbT[:, 1, g * 128:(g + 1) * 128], PT[3][:, g * 128:(g + 1) * 128], True, True,
                       [xbTb[1], PTb[3]], [psb[g]], True)
                else:
                    MM(o_, xbT[:, tb, g * 128:(g + 1) * 128], pprev[:, g, :], True, False,
                       [xbTb[tb], cstbb], [psb[g]], False)
                    MM(o_, xbT[:, 1 + tb, g * 128:(g + 1) * 128], pcur[:, g, :], False, True,
                       [xbTb[1 + tb], cstbb], [psb[g]], True)
            CP(diffT[g][:, :], psum[g][:, :], [psb[g]], [dfb[g]], eng="act")
        for g in range(4):
            MM(psum[4 + g][:, :], bw[:, g, :], diffT[g][:, :], True, True, [bwb, dfb[g]], [psb[4 + g]], True)
            STT(catT[:, 4 + g, :], psum[4 + g][:, :], col(g, R_BSC + i), sgB[:, g, :], ALU.mult, ALU.mult,
                [psb[4 + g], colsb, sgBb[g]], [catb[4 + g]])
        if L == layers[-1] and st + 1 < NST:
            prefetch_hook[0](st + 1, (0, 1, 2))
        mem_attn(L, [4, 5], [6, 7], [0, 1, 2, 3], part=(None if first else 2))
        out_proj(L, L != layers[-1])

    def build_diag_now(i):
        r0 = R_DW + 32 * i
        for c in range(4):
            for j0 in range(0, 31, 8):
                nj = min(8, 31 - j0)
                bg.append((lambda e, c=c, j0=j0, nj=nj: e.tensor_tensor(
                    out=diag[:, c * 31 + j0:c * 31 + j0 + nj, :],
                    in0=identh.unsqueeze(1).broadcast_to([128, nj, 128]),
                    in1=cols[:, c, r0 + j0:r0 + j0 + nj].unsqueeze(2).broadcast_to([128, nj, 128]), op=ALU.mult),
                    [cstfb, colsb], [diagb]))

    odd_seq = [(st, L) for st in range(NST) for L in layers if L % 2 == 1]
    odd_pos = [0]

    def layer_odd(L, st):
        i = L // 2
        first = (st == 0)
        base = WIN_BASE[L]
        DMA("pool", "pw", pw[:, :, :].rearrange("p a b -> p (a b)"), pwp_d[i], (), [pwb])
        rms_norm_x(L)
        MEMSET(Vz[:, :, :, :], 0.0, Vzb)
        if not first:
            CP(Kz[:, :, 0:128], carryK[i][:, :, :], [cKb[i]], [Kzb[0]])
            CP(Vz[:, 0, :, :], carryV[i][:, :, :], [cVb[i]], [Vzb[0]])
        CP(h1T[:, :, 0:32], carryH[i][:, :, :], [cHb[i]], h1b)
        wt, wbuf = w_get("win", base + ODD_G["db"])
        inproj_fm(wt, wbuf, range(4), [0, 1, 2, 3], kouter=True)
        for c in range(4):
            ACT(hn[:, c, :], psum[c][:, :], AF.Tanh, [psb[c]], [hnb[c]], scale=0.5)
        wt, wbuf = w_get("win", base + ODD_G["da"])
        inproj_fm(wt, wbuf, range(4), [4, 5, 6, 7])
        for c in range(4):
            STT(h1T[:, c, 32:32 + ST], hn[:, c, :], 1.0, psum[4 + c][:, :], ALU.add, ALU.mult,
                [hnb[c], psb[4 + c]], [h1b[c]])
        CP(carryH[i][:, :, :], h1T[:, :, ST:ST + 32], h1b, [cHb[i]])
        bg_flush(10000)
        filler = []
        for c in range(4):
            for j in range(31):
                filler.append((c, j))

        def fill(n):
            while filler and n > 0:
                c, j = filler.pop(0)
                MM(psum[4 + c][:, :], diag[:, c * 31 + j, :], h1T[:, c, 2 + j:2 + j + ST], j == 0, j == 30,
                   [diagb, h1b[c]], [psb[4 + c]], j == 30)
                n -= 1

        prefetch_wout()
        wt, wbuf = w_get("win", base + ODD_G["gc"])
        inproj_fm(wt, wbuf, range(4), [0, 1, 2, 3])
        for c in range(4):
            ACT(sgA[:, c, :], psum[c][:, :], AF.Silu, [psb[c]], [sgAb[c]])
        fill(4)
        wt, wbuf = w_get("win", base + ODD_G["gd"])
        inproj_fm(wt, wbuf, range(4), [0, 1, 2, 3])
        for c in range(4):
            ACT(sgB[:, c, :], psum[c][:, :], AF.Silu, [psb[c]], [sgBb[c]])
        fill(4)
        wt, wbuf = w_get("win", base + ODD_G["m"])
        inproj_fm(wt, wbuf, range(4), [0, 1, 2, 3])
        fill(3)
        mem_head(L, [0, 1], [2, 3])
        fill(12)
        wt, wbuf = w_get("win", base + ODD_G["kv"])
        inproj_fm(wt, wbuf, [0, 1], [0, 1])
        for tb in range(4):
            for k in range(KC):
                MM(psum[2][:, tb * 128:(tb + 1) * 128], xnT[:, k, tb * 128:(tb + 1) * 128], wt[:, k, 256:384],
                   k == 0, k == KC - 1, [wbuf[1], xnb[k]], [psb[2]], k == KC - 1)
        vzv = Vz[:, 1:5, :, :].rearrange("p s (kv j) c -> p s kv j c", j=2)
        pvv = psum[2][:, :].rearrange("p (t kv d) -> p t kv d", t=4, kv=2)
        for j in range(2):
            CP(vzv[:, :, :, j, 64 * j:64 * j + 64], pvv, [psb[2]], Vzb[1:5], eng="act")
        fill(10)
        items = []
        for kv in range(2):
            outs = []
            for j in range(2):
                psl = slice(64 * j, 64 * j + 64)
                outs.append((Kz[psl, kv * 2 + j, 128:640], psl, Kzb[1:5]))
            items.append((kv, 3 - kv, kv, kv, R_CK + i, outs))
        head_norm_multi(items)
        fill(12)
        wt, wbuf = w_get("win", base + ODD_G["q"])
        for qh in range(2):
            inproj_fm(wt, wbuf, [2 * qh, 2 * qh + 1], [0, 1])
            fill(6)
            head_norm_multi([(cc, 2 + cc, cc, cc, R_CQ + i,
                              [(qnT[:, 2 * qh + cc, :], slice(0, 128), [qnb[2 * qh + cc]])]) for cc in range(2)])
            fill(12)
        if L == layers[-1] and st + 1 < NST:
            prefetch_hook[0](st + 1, (1, 2))
        nfill_blk = max(0, len(filler) // 4)
        for n in range(4):
            kbs = []
            if not (first and n == 0):
                kbs.append((n, mneg_prev))
            kbs.append((n + 1, mneg_cur))
            pts = []
            for ki, (slot, mneg) in enumerate(kbs):
                pti = (n * 2 + ki) % 4
                pts.append(pti)
                for half in range(2):
                    b = half
                    MM(psum[b][:, :].rearrange("p (h t) -> p h t", h=4), identb,
                       mneg.unsqueeze(1).broadcast_to([128, 4, 128]), True, False, [cstbb], [psb[b]], False)
                    for hh in range(4):
                        h = half * 4 + hh
                        MM(psum[b][:, hh * 128:(hh + 1) * 128],
                           Kz[:, (h // 4) * 2 + (h % 2), slot * 128:(slot + 1) * 128],
                           qnT[:, h // 2, n * 128:(n + 1) * 128], False, hh == 3,
                           [Kzb[slot], qnb[h // 2]], [psb[b]], hh == 3)
                    ACT(PT[pti][:, half * 512:(half + 1) * 512], psum[b][:, :], AF.Exp,
                        [psb[b]], [PTb[pti]], scale=0.125)
                fill(4)
            ob, db_ = 2, 3
            for which, bank in (("o", ob), ("d", db_)):
                for c in range(4):
                    nmm = 2 * len(kbs)
                    m = 0
                    for ki, (slot, mneg) in enumerate(kbs):
                        ptv = PT[pts[ki]][:, :].rearrange("p (h t) -> p h t", h=8)
                        for j in range(2):
                            if which == "o":
                                lhsT = Vz[:, slot, (c // 2) * 2 + j, :]
                                Rr = [Vzb[slot], PTb[pts[ki]]]
                            else:
                                lhsT = opad[:, j, :]
                                Rr = [cstbb, PTb[pts[ki]]]
                            MM(psum[bank][:, c * 128:(c + 1) * 128], lhsT, ptv[:, 2 * c + j, :], m == 0, m == nmm - 1,
                               Rr, [psb[bank]], (m == nmm - 1))
                            m += 1
            fill(nfill_blk - 4 * len(kbs))
            dv = psum[db_][:, :].rearrange("p (c t) -> p c t", c=4)
            TT(fs[0][:, :].rearrange("p (c t) -> p c t", c=4), dv,
               esk[:, i, :].unsqueeze(2).broadcast_to([128, 4, 128]), ALU.add, [psb[db_], eskb], [fsb[0]])
            recip_act(fs[1][:, :], fs[0][:, :], [fsb[0]], [fsb[1]])
            TT(fs[2][:, :], psum[ob][:, :], fs[1][:, :], ALU.mult, [psb[ob], fsb[1]], [fsb[2]])
            TT(catT[:, 0:4, n * 128:(n + 1) * 128], fs[2][:, :].rearrange("p (c t) -> p c t", c=4),
               sgA[:, :, n * 128:(n + 1) * 128], ALU.mult, [fsb[2]] + sgAb, catb[0:4])
        def ln_act(c):
            s0, s1 = 2 * (c % 2), 2 * (c % 2) + 1
            ACT(sq[s0][:, :], psum[4 + c][:, :], AF.Identity, [psb[4 + c], colsb], [sqb[s0]], bias=col(c, R_DWB + i))
            ACT(sq[s1][:, :], psum[4 + c][:, :], AF.Square, [psb[4 + c], colsb], [sqb[s1]], bias=col(c, R_DWB + i))

        def ln_mm(c):
            s0, s1 = 2 * (c % 2), 2 * (c % 2) + 1
            MM(psum[0][:, :], o512, sq[s0][:, :], c == 0, c == 3, [cstbb, sqb[s0]], [psb[0]], True)
            MM(psum[1][:, :], o512, sq[s1][:, :], c == 0, c == 3, [cstbb, sqb[s1]], [psb[1]], True)

        fill(max(0, len(filler) - 31))
        CP(carryK[i][:, :, :], Kz[:, :, 512:640], [Kzb[4]], [cKb[i]])
        CP(carryV[i][:, :, :], Vz[:, 4, :, :], [Vzb[4]], [cVb[i]])
        ln_act(0)
        ln_act(1)
        fill(16)
        ln_mm(0)
        ln_act(2)
        fill(1000)
        if L == layers[-1] and st + 1 < NST:
            prefetch_hook[0](st + 1, (0,))
        odd_pos[0] += 1
        if odd_pos[0] < len(odd_seq):
            build_diag_now(odd_seq[odd_pos[0]][1] // 2)
        ln_mm(1)
        ln_act(3)
        ln_mm(2)
        ln_mm(3)
        CP(fs[0][:, :], psum[0][:, :], [psb[0]], [fsb[0]])
        TT(fs[1][:, :], fs[0][:, :], fs[0][:, :], ALU.mult, [fsb[0]], [fsb[1]])
        TT(fs[1][:, :], psum[1][:, :], fs[1][:, :], ALU.subtract, [psb[1], fsb[1]], [fsb[1]])
        rsqrt_eps(fs[1][:, :], fs[1][:, :], [fsb[1]], [fsb[1]])
        mem_attn(L, [4, 5], [6, 7], [2, 3, 0, 1], part=1)
        for c in range(4):
            t = fs[2 + c % 2]
            tb_ = fsb[2 + c % 2]
            STT(t[:, :], psum[4 + c][:, :], col(c, R_DWB + i), fs[0][:, :], ALU.add, ALU.subtract,
                [psb[4 + c], colsb, fsb[0]], [tb_])
            TT(t[:, :], t[:, :], fs[1][:, :], ALU.mult, [tb_, fsb[1]], [tb_])
            ACT(hn[:, c, :], t[:, :], AF.Silu, [tb_, colsb], [hnb[c]], bias=col(c, R_DLNB + i), scale=col(c, R_DLNG + i))
        mem_attn(L, [2, 3], [0, 1], [2, 3, 0, 1], part=2)
        odd_border = (2, 3, 0, 1, 4, 5, 6, 2)
        for dc in range(4):
            for kc in range(4):
                MM(psum[4 + dc][:, :], pw[:, kc, dc * 128:(dc + 1) * 128], hn[:, kc, :], kc == 0, kc == 3,
                   [pwb, hnb[kc]], [psb[4 + dc]], kc == 3)
            TT(catT[:, 4 + dc, :], psum[4 + dc][:, :], sgB[:, dc, :], ALU.mult, [psb[4 + dc], sgBb[dc]], [catb[4 + dc]])
        if L == layers[-1] and st + 1 < NST:
            prefetch_hook[0](st + 1, (3,))
            x3_prefetched[0] = True
        out_proj(L, L != layers[-1], border=odd_border, korder=(0, 1, 2, 3, 8, 9, 4, 5, 6, 7))

    def x_dma(st, tt, si):
        tok = st * ST + tt * 128
        DMA("sp", f"stage{si}", stage[si], x_d[tok:tok + 128, :], (), stageb[si])

    hts = [hT[:, 2 * mt:2 * mt + 2, :].rearrange("p a b -> p (a b)") for mt in range(2)]
    htsb = [hTb[0:2], hTb[2:4]]
    PTf = [PT[h][:, :].bitcast(F32) for h in range(4)]
    DMA("sp", "cstf", cstf[:, :], cstf_d, (), [cstfb])
    DMA("pool", "cstb", cstb[:, :], cstb_d, (), [cstbb])
    DMA("sp", "stage0", stage[0], prow_d, (), stageb[0])
    for mt in range(2):
        DMA("sp", f"hts{mt}", hts[mt], mem_d[mt * 128:(mt + 1) * 128, :], (), htsb[mt])
    sgu_tmp = [([PTf[0], PTf[1]], [PTb[0], PTb[1]], [PTf[2], PTf[3]], [PTb[2], PTb[3]], "ptf"),
               ([fs[0][:, :], fs[1][:, :]], [fsb[0], fsb[1]], [fs[2][:, :], fs[3][:, :]], [fsb[2], fsb[3]], "fs")]
    for i in range(2):
        wr_, wrb_, ab_, abb_, kp = sgu_tmp[i]
        for hh in range(2):
            DMA("sp", f"{kp}{hh}", wr_[hh], wstp_d[i][:, hh * 512:(hh + 1) * 512], (), [wrb_[hh]])
            DMA("sp", f"{kp}{2 + hh}", ab_[hh], absb_d[i][:, hh * 512:(hh + 1) * 512], (), [abb_[hh]])
    x_dma(0, 0, 2)
    x_dma(0, 2, 1)
    x_dma(0, 3, 3)
    x3_prefetched[0] = True
    for L in range(3):
        DMA("pool", f"win{L}h0", win[L][:, :, :].rearrange("p a b -> p (a b)"), wkvp_d[L], (), winb[L])
    for ap_, b_ in ((Kz[:, :, :], Kzb), (KTpad[:, :, :, :], KTb), (Vpad[:, :, :, :, :], Vpb),
                    (carryH[0][:, :, :], [cHb[0]]), (carryH[1][:, :, :], [cHb[1]])):
        MEMSET(ap_, 0.0, list(b_))
    for half in range(2):
        for j in range(4):
            c = half * 4 + j
            TR(psum[half][:, j * 128:(j + 1) * 128], stage[0][:, c * 128:(c + 1) * 128], ident,
               stageb[0] + [cstfb], [psb[half]], j == 3)
        CP(cols[:, half * 4:(half + 1) * 4, :].rearrange("p a b -> p (a b)"), psum[half][:, :], [psb[half]], [colsb],
           eng=("act" if half == 0 else "dve"))
    x_dma(0, 1, 0)
    for i in range(2):
        ACT(esk[:, i, :], cols[:, 0:4, R_SINK + i], AF.Exp, [colsb], [eskb])
    for i in range(2):
        wr_, wrb_, ab_, abb_, kp = sgu_tmp[i]
        for hh in range(2):
            wv = wr_[hh].rearrange("p (h t) -> p h t", h=4)
            TT(wv, wv, mcur.unsqueeze(1).broadcast_to([128, 4, 128]), ALU.mult, [wrb_[hh], cstbb], [wrb_[hh]])
            MM(psum[4 + 2 * i + hh][:, :], onesf, wr_[hh], True, True, [cstfb, wrb_[hh]], [psb[4 + 2 * i + hh]], True)
            CP(wsT2[i][:, hh * 4:(hh + 1) * 4, :].rearrange("p a b -> p (a b)"), wr_[hh], [wrb_[hh]], [wsTb2[i]], eng="act")
        for p in range(4):
            for j in range(2):
                h = 2 * p + j
                hh, hl = divmod(h, 4)
                psl = slice(64 * j, 64 * j + 64)
                STT(BT2[i][psl, p, :], psum[4 + 2 * i + hh][psl, hl * 128:(hl + 1) * 128],
                    cols[psl, p, R_ALNB + i:R_ALNB + i + 1], ab_[hh][psl, hl * 128:(hl + 1) * 128],
                    ALU.mult, ALU.add, [psb[4 + 2 * i + hh], colsb, abb_[hh]], [BTb2[i]])
    memnT = catT[:, 0:4, :].rearrange("p a b -> p (a b)").rearrange("p (k m) -> p k m", k=KC)
    for mt in range(2):
        sg_, sgb_ = hts[mt], htsb[mt]
        ACT(fs[0][:, :] if mt == 0 else fs[2][:, :], sg_[:, 0:512], AF.Square, sgb_, [fsb[0] if mt == 0 else fsb[2]])
        ACT(fs[1][:, :] if mt == 0 else fs[3][:, :], sg_[:, 512:1024], AF.Square, sgb_, [fsb[1] if mt == 0 else fsb[3]])
        f0, f0b = (fs[0], fsb[0]) if mt == 0 else (fs[2], fsb[2])
        f1, f1b = (fs[1], fsb[1]) if mt == 0 else (fs[3], fsb[3])
        st_, stb_ = stt[mt], sttb[mt]
        TT(f0[:, :], f0[:, :], f1[:, :], ALU.add, [f0b, f1b], [f0b])
        sc.op("dve", lambda e, st_=st_, f0=f0: e.reduce_sum(out=st_[:, 2:3], in_=f0[:, :], axis=mybir.AxisListType.X),
              [f0b], [stb_])
        TS(st_[:, 2:3], st_[:, 2:3], 1.0 / D, None, ALU.mult, None, [stb_], [stb_])
        rsqrt_eps(st_[:, 3:4], st_[:, 2:3], [stb_], [stb_])
        TS(sg_, sg_, st_[:, 3:4], None, ALU.mult, None, sgb_ + [stb_], sgb_)
        for half in range(2):
            bk = 2 * mt + half
            for j in range(4):
                k = half * 4 + j
                TR(psum[bk][:, j * 128:(j + 1) * 128], sg_[:, k * 128:(k + 1) * 128], ident,
                   sgb_ + [cstfb], [psb[bk]], j == 3)
            for j in range(4):
                k = half * 4 + j
                TS(memnT[:, k, mt * 128:(mt + 1) * 128], psum[bk][:, j * 128:(j + 1) * 128], col(k, R_MEMG), None,
                   ALU.mult, None, [psb[bk], colsb], catb[0:4])
    for L in range(4):
        slot = L % 3
        wk, wkb = win[slot], winb[slot]
        if L == 3:
            DMA("pool", f"win{slot}h0", wk[:, :, :].rearrange("p a b -> p (a b)"), wkvp_d[L], (), wkb)
        for p in range(2):
            for k in range(KC):
                MM(psum[p][:, 0:NMEM], wk[:, k, p * 128:(p + 1) * 128], memnT[:, k, :], k == 0, k == KC - 1,
                   wkb + catb[0:4], [psb[p]], k == KC - 1)
            ACT(sq[p][:, 0:NMEM], psum[p][:, 0:NMEM], AF.Square, [psb[p]], [sqb[p]])
            MM(psum[4 + p][:, 0:NMEM], b64, sq[p][:, 0:NMEM], True, True, [cstbb, sqb[p]], [psb[4 + p]], True)
            rsqrt_eps(fs[p][:, 0:NMEM], psum[4 + p][:, 0:NMEM], [psb[4 + p]], [fsb[p]])
            for j in range(2):
                psl = slice(64 * j, 64 * j + 64)
                STT(KTpad[psl, L, 2 * p + j, :], psum[p][psl, 0:NMEM], cols[psl, 0, R_MK + L:R_MK + L + 1],
                    fs[p][psl, 0:NMEM], ALU.mult, ALU.mult, [psb[p], colsb, fsb[p]], [KTb[L]])
        for mh in range(2):
            b = 2 + mh
            for k in range(KC):
                MM(psum[b][:, 0:256], memnT[:, k, mh * 128:(mh + 1) * 128], wk[:, k, 256:512], k == 0, k == KC - 1,
                   wkb + catb[0:4], [psb[b]], k == KC - 1)
            vv = Vpad[:, L, mh, :, :].rearrange("p (q j) c -> p q j c", j=2)
            pvv = psum[b][:, 0:256].rearrange("p (q j d) -> p q j d", q=2, j=2)
            for j in range(2):
                CP(vv[:, :, j, 64 * j:64 * j + 64], pvv[:, :, j, :], [psb[b]], [Vpb[L]], eng=("act" if j == 0 else "dve"))
    if any(L % 2 == 1 for L in layers):
        build_diag_now(odd_seq[0][1] // 2)

    def prefetch_x(st, tiles=(0, 1, 2)):
        for tt in tiles:
            x_dma(st, tt, (2, 0, 1, 3)[tt])

    prefetch_hook[0] = prefetch_x

    for st in range(NST):
        for tt in range(4):
            si = (2, 0, 1, 3 if x3_prefetched[0] else 2)[tt]
            if tt == 3:
                if not x3_prefetched[0]:
                    x_dma(st, 3, 2)
                x3_prefetched[0] = False
            for half in range(2):
                b = (tt * 2 + half) % 8
                for j in range(4):
                    k = half * 4 + j
                    TR(psum[b][:, j * 128:(j + 1) * 128], stage[si][:, k * 128:(k + 1) * 128], ident,
                       stageb[si] + [cstfb], [psb[b]], j == 3)
                CP(hT[:, half * 4:(half + 1) * 4, tt * 128:(tt + 1) * 128],
                   psum[b][:, :].rearrange("p (j t) -> p j t", j=4), [psb[b]], hTb[half * 4:(half + 1) * 4],
                   eng=("act" if half == 0 else "dve"))
        for L in layers:
            if L % 2 == 0:
                layer_even(L, st)
            else:
                layer_odd(L, st)
        for tt in range(4):
            tok = st * ST + tt * 128
            for half in range(2):
                b = (tt * 2 + half) % 8
                hi = tt * 2 + half
                for j in range(4):
                    k = half * 4 + j
                    TR(psum[b][:, j * 128:(j + 1) * 128], hT[:, k, tt * 128:(tt + 1) * 128], ident,
                       [hTb[k], cstfb], [psb[b]], j == 3)
                if hi < 4:
                    sap, sbf, key = fs[hi][:, :], [fsb[hi]], f"fs{hi}"
                else:
                    c0 = 2 * (hi - 4)
                    sap = catT[:, c0:c0 + 2, :].rearrange("p a b -> p (a b)").bitcast(F32)
                    sbf, key = catb[c0:c0 + 2], f"cst{hi}"
                CP(sap, psum[b][:, :], [psb[b]], sbf, eng=("act" if half == 0 else "dve"))
                ev = DMA("sp", key, out_d[tok:tok + 128, half * 512:(half + 1) * 512], sap, sbf, ())
                sc.final.append(ev)

    last = {}
    for key, sem, val in sc.final:
        last[key] = (sem, val)
    sc.q["sp"].append((list(last.values()), lambda e: e.nop(), None))

    with nc.Block() as block:
        @block.sync
        def _(e):
            sc.emit("sp", e)

        @block.tensor
        def _(e):
            sc.emit("pe", e)

        @block.vector
        def _(e):
            sc.emit("dve", e)

        @block.scalar
        def _(e):
            sc.emit("act", e)

        @block.gpsimd
        def _(e):
            sc.emit("pool", e)
    return nc, sc


def _pack_inputs(inp):
    f = lambda a: np.ascontiguousarray(np.asarray(a, dtype=np.float32))
    P = {}
    cstf = np.zeros((128, CF_N), np.float32)
    cstf[:, CF_IDENT:CF_IDENT + 128] = np.eye(128)
    cstf[:, CF_ONES:CF_ONES + 128] = 1.0
    cstf[:, CF_IDH:CF_IDH + 128] = 0.5 * np.eye(128)
    cstb = np.zeros((128, CB_N), np.float32)
    s_ = np.arange(128)[:, None]
    t_ = np.arange(128)[None, :]
    cstb[:, CB_MCUR:CB_MCUR + 128] = (t_ >= s_)
    cstb[:, CB_MPREV:CB_MPREV + 128] = (s_ > t_)
    cstb[:, CB_B64:CB_B64 + 128] = ((s_ // 64) == (t_ // 64)) / 64.0
    cstb[:, CB_O1024:CB_O1024 + 128] = 1.0 / 1024.0
    cstb[:, CB_O512:CB_O512 + 128] = 1.0 / 512.0
    cstb[:, CB_OPAD:CB_OPAD + 64] = 1.0
    cstb[:, CB_OPAD + 128 + 64:CB_OPAD + 256] = 1.0
    cstb[:, CB_IDB:CB_IDB + 128] = np.eye(128)
    cstb[:, CB_NCUR:CB_NCUR + 128] = np.where(t_ >= s_, 0.0, -2400.0)
    cstb[:, CB_NPREV:CB_NPREV + 128] = np.where(s_ > t_, 0.0, -2400.0)
    pfirst = np.zeros((128, 512), np.float32)
    for g in range(4):
        w = 2 << g
        inwin = ((t_ - s_) >= 0) & ((t_ - s_) < w)
        cstb[:, CB_PCUR + g * 128:CB_PCUR + (g + 1) * 128] = np.where(inwin, 1.0 / w, 0.0) - (s_ == t_)
        cstb[:, CB_PPREV + g * 128:CB_PPREV + (g + 1) * 128] = np.where((t_ + 128 - s_) < w, 1.0 / w, 0.0)
        pfirst[:, g * 128:(g + 1) * 128] = np.where(inwin, 1.0 / np.minimum(t_ + 1, w), 0.0) - (s_ == t_)
    P["cstf"], P["cstb"], P["pfirst"] = cstf, cstb, pfirst
    prow = np.zeros((128, D), np.float32)
    prow[R_NORM:R_NORM + 4] = f(inp["norm_g"])
    prow[R_MEMG] = f(inp["mem_norm_g"])
    for i in range(2):
        prow[R_ALNG + i, :512] = f(inp["a_ln_g"])[i]
        prow[R_ALNB + i, :512] = f(inp["a_ln_b"])[i]
        prow[R_BSC + i, :512] = f(inp["b_scale"])[i]
        prow[R_DWB + i, :512] = f(inp["d_dw_b"])[i]
        prow[R_DLNG + i, :512] = f(inp["d_ln_g"])[i]
        prow[R_DLNB + i, :512] = f(inp["d_ln_b"])[i]
        prow[R_CQ + i, :128] = np.tile(f(inp["c_qnorm"])[i], 2)
        prow[R_CK + i, :128] = np.tile(f(inp["c_knorm"])[i], 2)
        prow[R_SINK + i, :512] = np.repeat(f(inp["c_sink"])[i], 64)
        prow[R_DW + 32 * i:R_DW + 32 * i + 31, :512] = f(inp["d_dw"])[i]
    for L in range(4):
        prow[R_MQ + L, :128] = np.tile(f(inp["m_qnorm"])[L], 2)
        prow[R_MK + L, :128] = np.tile(f(inp["m_knorm"])[L], 2)
    P["prow"] = prow

    def pk(w, colidx):
        sub = w[:, colidx].reshape(KC, 128, len(colidx)).transpose(1, 0, 2)
        return np.ascontiguousarray(sub).reshape(128, -1)

    winp = np.zeros((26, 128, KC * 512), np.float32)
    for L in range(4):
        i = L // 2
        if L % 2 == 0:
            w = f(inp["w_in_even"])[i]
            for g in range(6):
                winp[WIN_BASE[L] + g] = pk(w, np.arange(g * 512, (g + 1) * 512))
        else:
            w = f(inp["w_in_odd"])[i]
            segs = {"q": (0, 512), "gc": (768, 1280), "da": (1280, 1792), "db": (1792, 2304), "gd": (2304, 2816),
                    "m": (2816, 3328)}
            for nm, (a, b) in segs.items():
                winp[WIN_BASE[L] + ODD_G[nm]] = pk(w, np.arange(a, b))
            idx = np.concatenate([np.arange(512, 576), np.arange(512, 576), np.arange(576, 640), np.arange(576, 640),
                                  np.arange(640, 768)])
            tmp = np.zeros((128, KC, 512), np.float32)
            tmp[:, :, :384] = pk(w, idx).reshape(128, KC, 384)
            winp[WIN_BASE[L] + ODD_G["kv"]] = tmp.reshape(128, -1)
    P["winp"] = winp
    woutp = np.zeros((16, 128, 10 * 256), np.float32)
    for L in range(4):
        w = f(inp["w_out_even"] if L % 2 == 0 else inp["w_out_odd"])[L // 2]
        for dp in range(4):
            sub = w[:, dp * 256:(dp + 1) * 256].reshape(10, 128, 256).transpose(1, 0, 2)
            woutp[L * 4 + dp] = np.ascontiguousarray(sub).reshape(128, -1)
    P["woutp"] = woutp
    P["wkvp"] = np.stack([pk(f(inp["w_mem_kv"])[L], np.arange(512)) for L in range(4)])
    P["pwp"] = np.stack([np.ascontiguousarray(f(inp["d_pw"])[i].reshape(4, 128, 512).transpose(1, 0, 2)).reshape(128, -1)
                         for i in range(2)])
    P["bwp"] = np.stack([np.ascontiguousarray(f(inp["b_w"])[i].transpose(1, 0, 2)).reshape(128, -1) for i in range(2)])
    P["wstp"] = np.stack([np.ascontiguousarray(f(inp["a_ws"])[i].transpose(2, 0, 1)).reshape(128, -1) for i in range(2)])
    P["absb"] = np.stack([np.ascontiguousarray(np.broadcast_to(f(inp["a_bs"])[i].reshape(1, -1), (128, 1024)))
                          for i in range(2)])
    return P


_CACHE = {}


def run_layers(inputs, layers, trace=False):
    key = tuple(layers)
    if key not in _CACHE:
        _CACHE[key] = build_program(list(layers))
    nc, sc = _CACHE[key]
    x = np.ascontiguousarray(np.asarray(inputs["x"], dtype=np.float32))
    mem = np.ascontiguousarray(np.asarray(inputs["mem"], dtype=np.float32))
    P = _pack_inputs(inputs)
    in_maps = []
    for b in range(NCORES):
        m = {"x": x[b], "mem": mem[b]}
        m.update(P)
        in_maps.append(m)
    res = run_bass_kernel_spmd(nc, in_maps, core_ids=list(range(NCORES)), trace=trace)
    outp = np.stack([np.asarray(r["out"]) for r in res.results], axis=0)
    return outp.astype(np.float32), res


def kernel(**inputs):
    outp, _ = run_layers(inputs, (0, 1, 2, 3))
    return outp
```
